# Optimizing a Trainium2 kernel written in Bass

```python
import jax, jax.numpy as jnp
from jax import lax
import numpy as np

D_MODEL = 1024
BATCH = 16
SEQ = 256
DEPTH = 4
DEC_BATCH = 8
DEC_SEQ = 4096
PAST_LEN = 512

GRID_W = 64
D_POOL = 512
N_POOL_GROUPS = 4
POOL_GROUP_W = D_POOL // N_POOL_GROUPS
POOL_WINDOWS = (2, 4, 8, 16)
D_LRU = 512
N_LRU_HEADS = 8
LRU_HEAD_W = D_LRU // N_LRU_HEADS
CONV_W = 4
LRU_C = 8.0
D_MIX = D_POOL + D_LRU
D_IN = D_POOL + 2 * D_LRU
D_FF = 2816
N_MOD = 9
EPS = 1e-6

kernel_name = 'hymba_pool_rglru_macaron_prefix_dit'


def rms_norm(x, g):
    xf = x.astype(jnp.float32)
    y = xf * lax.rsqrt(jnp.mean(xf * xf, axis=-1, keepdims=True) + EPS)
    return (y * g.astype(jnp.float32)).astype(x.dtype)


def modulate(x, shift, scale):
    return x * (1 + scale) + shift


def swiglu(x, w_gate, w_up, w_down):
    return (jax.nn.silu(x @ w_gate) * (x @ w_up)) @ w_down


def multiscale_pool(u, w_pool, pool_scale):
    n, L, _ = u.shape
    uf = u.astype(jnp.float32)
    cs = jnp.concatenate([jnp.zeros((n, 1, D_POOL), jnp.float32), jnp.cumsum(uf, axis=1)], axis=1)
    t = jnp.arange(L)
    outs = []
    for g, w in enumerate(POOL_WINDOWS):
        lo = jnp.clip(t - w // 2, 0, L)
        hi = jnp.clip(t + w // 2, 0, L)
        sl = slice(g * POOL_GROUP_W, (g + 1) * POOL_GROUP_W)
        csg = cs[..., sl]
        s = jnp.take(csg, hi, axis=1) - jnp.take(csg, lo, axis=1)
        cnt = (hi - lo).astype(jnp.float32)[None, :, None]
        outs.append(s / cnt - uf[..., sl])
    pooled = jnp.stack(outs, axis=2).astype(u.dtype)
    mixed = jnp.einsum('nlgc,gcd->nlgd', pooled, w_pool).reshape(n, L, D_POOL)
    return mixed * pool_scale


def centred_conv(u, w, b):
    L = u.shape[1]
    up = jnp.pad(u, ((0, 0), (CONV_W // 2, CONV_W - 1 - CONV_W // 2), (0, 0)))
    out = b + up[:, 0:L] * w[0]
    for k in range(1, CONV_W):
        out = out + up[:, k:k + L] * w[k]
    return out


def block_diag(x, w, b):
    bsz, L, _ = x.shape
    y = jnp.einsum('blhi,hij->blhj', x.reshape(bsz, L, N_LRU_HEADS, LRU_HEAD_W), w)
    return y.reshape(bsz, L, D_LRU) + b


def rglru(x, w_r, b_r, w_i, b_i, lam, h0, reverse):
    xf = x.astype(jnp.float32)
    r = jax.nn.sigmoid(block_diag(xf, w_r.astype(jnp.float32), b_r.astype(jnp.float32)))
    i = jax.nn.sigmoid(block_diag(xf, w_i.astype(jnp.float32), b_i.astype(jnp.float32)))
    log_a = LRU_C * r * jax.nn.log_sigmoid(lam.astype(jnp.float32))
    a = jnp.exp(log_a)
    bv = jnp.sqrt(-jnp.expm1(2.0 * log_a)) * (i * xf)
    idx = -1 if reverse else 0
    bv = bv.at[:, idx].add(a[:, idx] * h0.astype(jnp.float32))

    def combine(e1, e2):
        a1, b1 = e1
        a2, b2 = e2
        return a1 * a2, a2 * b1 + b2

    _, h = lax.associative_scan(combine, (a, bv), axis=1, reverse=reverse)
    last = 0 if reverse else -1
    return h, h[:, last]


def mixer(h, h0, on_grid, w_in, conv_w, conv_b, w_pool, pool_scale, w_r, b_r, w_i, b_i, lam, w_out):
    bsz, L, _ = h.shape
    u = h @ w_in
    u_pool = u[..., :D_POOL]
    u_rec = u[..., D_POOL:D_POOL + D_LRU]
    u_gate = u[..., D_POOL + D_LRU:]
    if on_grid:
        rows = L // GRID_W
        pool_out = multiscale_pool(u_pool.reshape(bsz * rows, GRID_W, D_POOL), w_pool, pool_scale)
        pool_out = pool_out.reshape(bsz, L, D_POOL)
    else:
        pool_out = multiscale_pool(u_pool, w_pool, pool_scale)
    xc = centred_conv(u_rec, conv_w, conv_b)
    y_f, s_f = rglru(xc, w_r[0], b_r[0], w_i[0], b_i[0], lam[0], h0[:, 0], False)
    y_b, s_b = rglru(xc, w_r[1], b_r[1], w_i[1], b_i[1], lam[1], h0[:, 1], True)
    rec_out = (y_f + y_b).astype(h.dtype) * jax.nn.gelu(u_gate)
    out = jnp.concatenate([pool_out, rec_out], axis=-1) @ w_out
    return out, jnp.stack([s_f, s_b], axis=1).astype(h.dtype)


def run_layer(x, mod, h0, on_grid, norm_g, f1g, f1u, f1d, w_in, conv_w, conv_b, w_pool, pool_scale,
              w_r, b_r, w_i, b_i, lam, w_out, f2g, f2u, f2d):
    m = [mod[:, k][:, None, :] for k in range(N_MOD)]
    h = modulate(rms_norm(x, norm_g[0]), m[0], m[1])
    x = x + 0.5 * m[2] * swiglu(h, f1g, f1u, f1d)
    h = modulate(rms_norm(x, norm_g[1]), m[3], m[4])
    y, st = mixer(h, h0, on_grid, w_in, conv_w, conv_b, w_pool, pool_scale, w_r, b_r, w_i, b_i, lam, w_out)
    x = x + m[5] * y
    h = modulate(rms_norm(x, norm_g[2]), m[6], m[7])
    x = x + 0.5 * m[8] * swiglu(h, f2g, f2u, f2d)
    return x, st


def setup_inputs(seed: int = 0) -> dict:
    key = jax.random.key(seed)
    ks = jax.random.split(key, 32)
    f32 = jnp.float32
    nrm = lambda k, shape, s: jax.random.normal(k, shape, f32) * s
    u_a = jax.random.uniform(ks[20], (DEPTH, 2, D_LRU), f32, 0.9, 0.999)
    s_a = u_a ** (1.0 / LRU_C)
    lam = jnp.log(s_a) - jnp.log1p(-s_a)
    return {
        'x_prompt': nrm(ks[0], (BATCH, SEQ, D_MODEL), 1.0),
        'x_sample': nrm(ks[1], (DEC_BATCH, DEC_SEQ, D_MODEL), 1.0),
        'state_lru': nrm(ks[2], (DEC_BATCH, DEPTH, 2, D_LRU), 0.5),
        'c': nrm(ks[3], (DEC_BATCH, D_MODEL), 1.0),
        'c_ctx': nrm(ks[4], (D_MODEL,), 1.0),
        'norm_g': 1.0 + nrm(ks[5], (DEPTH, 3, D_MODEL), 0.05),
        'w_ada': nrm(ks[6], (DEPTH, D_MODEL, N_MOD * D_MODEL), 0.5 * D_MODEL ** -0.5),
        'b_ada': nrm(ks[7], (DEPTH, N_MOD * D_MODEL), 0.02),
        'ffn1_gate': nrm(ks[8], (DEPTH, D_MODEL, D_FF), D_MODEL ** -0.5),
        'ffn1_up': nrm(ks[9], (DEPTH, D_MODEL, D_FF), D_MODEL ** -0.5),
        'ffn1_down': nrm(ks[10], (DEPTH, D_FF, D_MODEL), D_FF ** -0.5),
        'w_in': nrm(ks[11], (DEPTH, D_MODEL, D_IN), D_MODEL ** -0.5),
        'conv_w': nrm(ks[12], (DEPTH, CONV_W, D_LRU), CONV_W ** -0.5),
        'conv_b': nrm(ks[13], (DEPTH, D_LRU), 0.02),
        'w_pool': nrm(ks[14], (DEPTH, N_POOL_GROUPS, POOL_GROUP_W, POOL_GROUP_W), POOL_GROUP_W ** -0.5),
        'pool_scale': 1.0 + nrm(ks[15], (DEPTH, D_POOL), 0.05),
        'lru_w_r': nrm(ks[16], (DEPTH, 2, N_LRU_HEADS, LRU_HEAD_W, LRU_HEAD_W), LRU_HEAD_W ** -0.5),
        'lru_b_r': nrm(ks[17], (DEPTH, 2, D_LRU), 0.02),
        'lru_w_i': nrm(ks[18], (DEPTH, 2, N_LRU_HEADS, LRU_HEAD_W, LRU_HEAD_W), LRU_HEAD_W ** -0.5),
        'lru_b_i': nrm(ks[19], (DEPTH, 2, D_LRU), 0.02),
        'lru_lambda': lam,
        'w_out': nrm(ks[21], (DEPTH, D_MIX, D_MODEL), D_MIX ** -0.5),
        'ffn2_gate': nrm(ks[22], (DEPTH, D_MODEL, D_FF), D_MODEL ** -0.5),
        'ffn2_up': nrm(ks[23], (DEPTH, D_MODEL, D_FF), D_MODEL ** -0.5),
        'ffn2_down': nrm(ks[24], (DEPTH, D_FF, D_MODEL), D_FF ** -0.5),
        'final_g': 1.0 + nrm(ks[25], (D_MODEL,), 0.05),
    }


def reference(x_prompt, x_sample, state_lru, c, c_ctx, norm_g, w_ada, b_ada, ffn1_gate, ffn1_up, ffn1_down,
              w_in, conv_w, conv_b, w_pool, pool_scale, lru_w_r, lru_b_r, lru_w_i, lru_b_i, lru_lambda,
              w_out, ffn2_gate, ffn2_up, ffn2_down, final_g):
    xp = x_prompt
    xs = x_sample
    h0_ctx = jnp.zeros((xp.shape[0], 2, D_LRU), xp.dtype)
    sc_ctx = jax.nn.silu(c_ctx)[None, :]
    sc_lat = jax.nn.silu(c)
    new_states = []
    for l in range(DEPTH):
        mod_ctx = (sc_ctx @ w_ada[l] + b_ada[l]).reshape(1, N_MOD, D_MODEL)
        mod_lat = (sc_lat @ w_ada[l] + b_ada[l]).reshape(sc_lat.shape[0], N_MOD, D_MODEL)
        lp = (norm_g[l], ffn1_gate[l], ffn1_up[l], ffn1_down[l], w_in[l], conv_w[l], conv_b[l],
              w_pool[l], pool_scale[l], lru_w_r[l], lru_b_r[l], lru_w_i[l], lru_b_i[l], lru_lambda[l],
              w_out[l], ffn2_gate[l], ffn2_up[l], ffn2_down[l])
        xp, st = run_layer(xp, mod_ctx, h0_ctx, False, *lp)
        new_states.append(st)
        xs, _ = run_layer(xs, mod_lat, state_lru[:, l], True, *lp)
    y_prompt = rms_norm(xp, final_g)
    y_sample = rms_norm(xs, final_g)
    new_state_lru = jnp.stack(new_states, axis=1)
    return (y_prompt, y_sample, new_state_lru)
```

```python
import contextlib
import numpy as np
import concourse.bass as bass
import concourse.mybir as mybir
from concourse.bass_utils import run_bass_kernel_spmd

F32 = mybir.dt.float32
BF16 = mybir.dt.bfloat16
F32R = mybir.dt.float32r
AF = mybir.ActivationFunctionType
ALU = mybir.AluOpType

D = 1024
KC = 8
DFF = 2816
FC = 22
DEPTH = 4
NT = 512
NTILES = 9
TTOK = NT * NTILES
DIN = 1536
DLRU = 512
EPS = 1e-6
N_CORES = 8
LP = 4620
UBASE_S = 518


class Buf:
    __slots__ = ("name", "w", "r", "strict")

    def __init__(self, name, strict=False):
        self.name = name
        self.w = None
        self.r = {}
        self.strict = strict


class _Eng:
    def __init__(self, name):
        self.name = name
        self.sem = None
        self.count = 0
        self.seen = {}
        self.ops = []
        self.dma_sems = []
        self.dma_vals = []
        self.dma_rr = 0


class Sched:
    ENGS = ("pe", "act", "dve", "pool", "sp")

    def __init__(self, nc, stack, n_dma_sems=None):
        self.nc = nc
        self.e = {n: _Eng(n) for n in self.ENGS}
        n_dma_sems = n_dma_sems or {"sp": 16, "pool": 10, "act": 4}
        for n in self.ENGS:
            self.e[n].sem = stack.enter_context(nc.semaphore("s_" + n))
        for q, k in n_dma_sems.items():
            for i in range(k):
                self.e[q].dma_sems.append(stack.enter_context(nc.semaphore(f"d_{q}{i}")))
                self.e[q].dma_vals.append(0)
        self.final_tokens = []
        self.n_ops = 0

    @staticmethod
    def _key(sem):
        return id(sem)

    def _collect(self, eng, reads, writes, drop_own):
        deps = {}
        def add(tok, strict):
            if tok is None:
                return
            sem, val = tok
            k = id(sem)
            if drop_own and not strict and sem is eng.sem:
                return
            if k not in deps or deps[k][1] < val:
                deps[k] = (sem, val)
        for b in reads:
            add(b.w, b.strict)
        for b in writes:
            add(b.w, b.strict)
            for t in b.r.values():
                add(t, b.strict)
        waits = []
        for k, (sem, val) in deps.items():
            if eng.seen.get(k, 0) >= val:
                continue
            eng.seen[k] = val
            waits.append((sem, val))
        return waits

    def _commit(self, tok, reads, writes):
        k = id(tok[0])
        for b in reads:
            b.r[k] = tok
        for b in writes:
            b.w = tok
            b.r = {}

    def op(self, engine, fn, reads=(), writes=(), strict=False):
        return self.group(engine, [fn], reads, writes, strict=strict)

    @staticmethod
    def inherit(new_bufs, old_bufs):
        toks = {}
        for b in old_bufs:
            for t in ([b.w] if b.w is not None else []) + list(b.r.values()):
                k = id(t[0])
                if k not in toks or toks[k][1] < t[1]:
                    toks[k] = t
        for nb in new_bufs:
            for k, t in toks.items():
                if k not in nb.r or nb.r[k][1] < t[1]:
                    nb.r[k] = t

    def group(self, engine, fns, reads=(), writes=(), strict=False):
        eng = self.e[engine]
        waits = self._collect(eng, reads, writes, drop_own=not strict)
        eng.count += 1
        tok = (eng.sem, eng.count)
        self._commit(tok, reads, writes)
        for i, fn in enumerate(fns):
            eng.ops.append((waits if i == 0 else (), fn, (eng.sem, 1) if i == len(fns) - 1 else None))
        self.n_ops += len(fns)
        return tok

    def dma(self, queue, out, in_, reads=(), writes=(), final=False, **kw):
        eng = self.e[queue]
        waits = self._collect(eng, reads, writes, drop_own=False)
        i = eng.dma_rr
        eng.dma_rr = (i + 1) % len(eng.dma_sems)
        sem = eng.dma_sems[i]
        prev = eng.dma_vals[i]
        k = id(sem)
        if prev > 0 and eng.seen.get(k, 0) < prev:
            eng.seen[k] = prev
            waits.append((sem, prev))
        val = prev + 16
        eng.dma_vals[i] = val
        tok = (sem, val)
        self._commit(tok, reads, writes)
        eng.ops.append((waits, (lambda e, out=out, in_=in_, kw=kw: e.dma_start(out=out, in_=in_, **kw)), (sem, 16)))
        self.n_ops += 1
        if final:
            self.final_tokens.append(tok)
        return tok

    def emit(self, block):
        sp = self.e["sp"]
        fin = []
        for sem, val in self.final_tokens:
            if sp.seen.get(id(sem), 0) < val:
                sp.seen[id(sem)] = val
                fin.append((sem, val))
        for n in ("pe", "act", "dve", "pool"):
            en = self.e[n]
            if en.count > 0:
                fin.append((en.sem, en.count))

        def run(eng_state, extra_waits=()):
            def body(e):
                for waits, fn, inc in eng_state.ops:
                    for sem, val in waits:
                        e.wait_ge(sem, val)
                    ins = fn(e)
                    if inc is not None:
                        ins.then_inc(inc[0], inc[1])
                for sem, val in extra_waits:
                    e.wait_ge(sem, val)
            return body

        block.tensor(run(self.e["pe"]))
        block.scalar(run(self.e["act"]))
        block.vector(run(self.e["dve"]))
        block.gpsimd(run(self.e["pool"]))
        block.sync(run(sp, fin))


class Arena:
    def __init__(self, nc, stack, name, nbytes):
        self.nbytes = nbytes
        self.t = stack.enter_context(nc.sbuf_tensor(name, [128, nbytes // 4], F32))

    def view(self, off, shape_free, dt):
        esz = {F32: 4, F32R: 4, BF16: 2}[dt]
        n = int(np.prod(shape_free))
        assert off % 4 == 0 and off + n * esz <= self.nbytes, (off, n * esz, self.nbytes)
        w = (n * esz + 3) // 4
        ap = self.t[:, off // 4: off // 4 + w]
        if dt != F32:
            ap = ap.bitcast(dt)
        if len(shape_free) == 1:
            return ap
        names = " ".join(f"a{i}" for i in range(len(shape_free)))
        kw = {f"a{i}": int(s) for i, s in enumerate(shape_free)}
        return ap.rearrange(f"p ({names}) -> p {names}", **kw)


def build_program(cfg=None):
    cfg = cfg or {}
    depth = cfg.get("depth", DEPTH)
    do_mixer = cfg.get("mixer", True)
    do_ffn = cfg.get("ffn", True)

    nc = bass.Bass("TRN2", target_bir_lowering=False)
    stack = contextlib.ExitStack()

    def din(name, shape, dt=F32):
        return nc.dram_tensor(name, list(shape), dt, kind="ExternalInput").ap()

    def dout(name, shape, dt=F32):
        return nc.dram_tensor(name, list(shape), dt, kind="ExternalOutput").ap()

    def dscr(name, shape, dt=F32):
        return nc.dram_tensor(name, list(shape), dt, kind=("ExternalOutput" if cfg.get("debug") else "Internal")).ap()

    xp = din("xp", [512, D])
    xs = din("xs", [4096, D])
    st_in = din("st", [32, 128])
    cc_in = din("cc", [16, 128])
    eye_in = din("eye", [128, 128])
    poolm_in = din("poolm", [128, 20, 128])
    norm_g = din("norm_g", [DEPTH, 24, 128])
    w_ada = din("w_ada", [DEPTH, D, 9 * D])
    b_ada = din("b_ada", [DEPTH, 72, 128])
    f1g = din("ffn1_gate", [DEPTH, D, DFF])
    f1u = din("ffn1_up", [DEPTH, D, DFF])
    f1d = din("ffn1_down", [DEPTH, DFF, D])
    w_in = din("w_in", [DEPTH, D, DIN])
    conv_w = din("conv_w", [DEPTH, 16, 128])
    conv_b = din("conv_b", [DEPTH, 4, 128])
    w_pool = din("w_pool", [DEPTH, 4, 128, 128])
    pool_scale = din("pool_scale", [DEPTH, 4, 128])
    lru_w_r = din("lru_w_r", [DEPTH, 2, 8, 64, 64])
    lru_b_r = din("lru_b_r", [DEPTH, 8, 128])
    lru_w_i = din("lru_w_i", [DEPTH, 2, 8, 64, 64])
    lru_b_i = din("lru_b_i", [DEPTH, 8, 128])
    lru_lam = din("lru_lambda", [DEPTH, 8, 128])
    w_out = din("w_out", [DEPTH, D, D])
    f2g = din("ffn2_gate", [DEPTH, D, DFF])
    f2u = din("ffn2_up", [DEPTH, D, DFF])
    f2d = din("ffn2_down", [DEPTH, DFF, D])
    final_g = din("final_g", [8, 128])
    yp = dout("yp", [512, D])
    ys = dout("ys", [4096, D])
    ns_out = dout("ns", [64, 128])
    xresT = dscr("xresT", [D, TTOK])
    urecT = dscr("urecT", [DLRU, LP])
    xcT = dscr("xcT", [DLRU, TTOK])
    yfT = dscr("yfT", [DLRU, TTOK])
    ggT = dscr("ggT", [DLRU, TTOK])
    poT = dscr("poT", [DLRU, TTOK], BF16)

    S = Sched(nc, stack)

    WBIG_B = 3 * 45056
    ACTV_B = 64512
    wbig = Arena(nc, stack, "wbig", WBIG_B)
    actv = Arena(nc, stack, "actv", ACTV_B)
    const = Arena(nc, stack, "const", 8192)
    ones_t = stack.enter_context(nc.sbuf_tensor("ones_r", [128, 128], F32R))
    ones_r = ones_t[:]
    sq_t = stack.enter_context(nc.sbuf_tensor("sq_r", [128, 2, NT], F32R))

    WD_OFF = 90112
    wg_sb = wbig.view(0, [KC, DFF], BF16)
    wu_sb = wbig.view(45056, [KC, DFF], BF16)
    wd_sb = wbig.view(WD_OFF, [FC, D], BF16)
    b_wg = [[Buf("wgA0"), Buf("wgA1")], [Buf("wgB0"), Buf("wgB1")]]
    b_wu = [[Buf("wuA0"), Buf("wuA1")], [Buf("wuB0"), Buf("wuB1")]]
    b_wd2 = [[Buf("wdL0"), Buf("wdL1")], [Buf("wdR0"), Buf("wdR1")]]
    b_wd = [x for h in b_wd2 for x in h]

    ident = const.view(0, [128], F32)
    pcols = const.view(512, [512], F32)
    stg = const.view(2560, [128], F32)
    stg2 = const.view(3072, [128], F32)
    scT = const.view(3584, [16], F32)
    nhalf = const.view(3648, [NT], F32)
    phalf = const.view(5696, [NT], F32)
    b_ident, b_ones, b_pcols, b_scT = Buf("ident"), Buf("ones"), Buf("pcols", True), Buf("scT", True)
    b_stg, b_stg2, b_half = Buf("stg", True), Buf("stg2", True), Buf("half")

    PC_MOD, PC_G, PC_GMOD, PC_SHIFT, PC_COEF, PC_FG = 0, 144, 168, 216, 264, 312
    MC = 320
    MC_CW, MC_CB, MC_PS, MC_BR, MC_BI, MC_LAM = MC + 0, MC + 16, MC + 20, MC + 24, MC + 32, MC + 40
    MC_NBR, MC_NBI, MC_HC = MC + 48, MC + 56, MC + 64
    MC_T = MC + 72
    PC_H0 = 424
    PC_NS = 456
    PC_CAR = 456
    PC_EPS = 464
    PC_ONE = 465

    def col(base, idx):
        return pcols[:, base + idx: base + idx + 1]

    psum = [stack.enter_context(nc.psum_tensor(f"ps{i}", [128, 512], F32))[:] for i in range(8)]
    b_ps = [Buf(f"ps{i}") for i in range(8)]
    b_modps = Buf("modps", True)

    o = 0
    xrot = [actv.view(o + i * 2048, [NT], F32) for i in range(2)]; o += 4096
    rstd = actv.view(o, [NT], F32); o += 2048
    tn = [actv.view(o, [NT], F32)]; o += 2048
    xnT = actv.view(o, [KC, NT], BF16); o += 8192
    NORM_END = o
    b_xrot = [Buf(f"xrot{i}") for i in range(2)]
    sq = [sq_t[:, i, :] for i in range(2)]
    b_sq = [Buf(f"sq{i}") for i in range(2)]
    b_rstd, b_tn, b_xnT = Buf("rstd"), [Buf("tn0")], Buf("xnT")
    o = NORM_END
    hT = actv.view(o, [FC, NT], BF16); o += 22528
    gsb = [actv.view(o + i * 1024, [NT], BF16) for i in range(2)]; o += 2048
    xrr = [actv.view(o + i * 2048, [NT], F32) for i in range(2)]; o += 4096
    xnew = [actv.view(o + i * 2048, [NT], F32) for i in range(2)]; o += 4096
    XRR_OFF = o - 8192
    stgw = [actv.view(o + i * 5632, [1408], F32) for i in range(2)]; o += 11264
    STGW_OFF = o - 11264
    assert o <= ACTV_B, o
    b_hT = Buf("hT")
    b_gsb = [Buf(f"gsb{i}") for i in range(2)]
    b_xrr = [Buf(f"xrr{i}") for i in range(2)]
    b_xnew = [Buf(f"xnew{i}") for i in range(2)]
    b_stgw = [Buf(f"stgw{i}") for i in range(2)]
    ffn_region_bufs = [b_hT] + b_gsb + b_xrr + b_xnew + b_stgw

    b_xres = [[Buf(f"xres{t}_{c}") for c in range(KC)] for t in range(NTILES)]
    b_urec = [[Buf(f"urec{t}_{c}") for c in range(4)] for t in range(NTILES)]
    b_xc = [[Buf(f"xc{t}_{c}") for c in range(4)] for t in range(NTILES)]
    b_yf = [[Buf(f"yf{t}_{c}") for c in range(4)] for t in range(NTILES)]
    b_gg = [[Buf(f"gg{t}_{c}") for c in range(4)] for t in range(NTILES)]
    b_po = [[Buf(f"po{t}_{c}") for c in range(4)] for t in range(NTILES)]
    b_upad = Buf("upad")

    rr = {}

    def nxt(name, n):
        v = rr.get(name, 0)
        rr[name] = (v + 1) % n
        return v

    def tile_cols(t):
        return slice(t * NT, (t + 1) * NT)

    region_all = {"norm": [], "mid": [], "wd": []}

    def claim(region, bufs):
        cur = region_all[region]
        ids = {id(x) for x in bufs}
        S.inherit(bufs, [x for x in cur if id(x) not in ids])
        have = {id(x) for x in cur}
        cur.extend(x for x in bufs if id(x) not in have)

    S.dma("sp", ident, eye_in, writes=[b_ident])
    S.op("pool", lambda e: e.memset(tn[0][:, 0:128], 1.0), writes=[b_tn[0]])
    S.op("dve", lambda e: e.tensor_copy(out=ones_r, in_=tn[0][:, 0:128]), reads=[b_tn[0]], writes=[b_ones])
    S.op("pool", lambda e: e.memset(nhalf, -0.5), writes=[b_half])
    S.op("pool", lambda e: e.memset(phalf, 0.5), writes=[b_half])
    S.op("pool", lambda e: e.memset(pcols, 0.0), writes=[b_pcols])
    S.op("pool", lambda e: e.memset(pcols[:, PC_EPS:PC_EPS + 1], EPS), writes=[b_pcols])
    S.op("pool", lambda e: e.memset(pcols[:, PC_ONE:PC_ONE + 1], 1.0), writes=[b_pcols])

    def transpose_rows(nrows, src_stage, b_src, dst_cols, b_dst, bank=7):
        S.op("pe", lambda e: e.transpose(psum[bank][:, 0:nrows], src_stage[0:nrows, :], ident[0:nrows, 0:nrows]),
             reads=[b_src, b_ident], writes=[b_ps[bank]])
        S.op("dve", lambda e: e.tensor_copy(out=dst_cols, in_=psum[bank][:, 0:nrows]),
             reads=[b_ps[bank]], writes=[b_dst])

    S.dma("sp", stg[0:16, :], cc_in, writes=[b_stg])
    S.op("act", lambda e: e.activation(out=stg[0:16, :], in_=stg[0:16, :], func=AF.Silu), reads=[b_stg], writes=[b_stg])
    transpose_rows(16, stg, b_stg, scT, b_scT)
    S.dma("sp", stg2[0:8, :], final_g, writes=[b_stg2])
    S.dma("sp", stg2[32:64, :], st_in, writes=[b_stg2])
    S.op("pe", lambda e: e.transpose(psum[7][:, 0:64], stg2[0:64, :], ident[0:64, 0:64]),
         reads=[b_stg2, b_ident], writes=[b_ps[7]])
    S.op("dve", lambda e: e.tensor_copy(out=pcols[:, PC_FG:PC_FG + 8], in_=psum[7][:, 0:8]), reads=[b_ps[7]], writes=[b_pcols])
    S.op("dve", lambda e: e.tensor_copy(out=pcols[:, PC_H0:PC_H0 + 32], in_=psum[7][:, 32:64]), reads=[b_ps[7]], writes=[b_pcols])

    nsT = const.view(7744, [64], F32)
    b_nsT = Buf("nsT", True)
    modb = stg2[:, 0:72]
    b_modb = b_stg2
    S.op("pool", lambda e: e.memset(nsT, 0.0), writes=[b_nsT])

    wa_st = [wbig.view(WD_OFF + 36864 + i * 4096, [KC, 128], F32) for i in range(2)]
    b_wa = [Buf("wa0"), Buf("wa1")]
    wa_all = list(b_wa)
    NPIECE = 72

    def mod_begin(l):
        claim("wd", wa_all)
        S.inherit([b_modps], [b_ps[6]])
        S.dma("sp", stg[0:72, :], b_ada[l], writes=[b_stg])
        S.dma("sp", stg[72:96, :], norm_g[l], writes=[b_stg])

    def mod_piece_dma(l, piece):
        slot = piece % 2
        S.dma("sp", wa_st[slot], w_ada[l, :, piece * 128:(piece + 1) * 128].rearrange("(k p) c -> p k c", p=128),
              writes=[b_wa[slot]])

    def mod_piece_mm(l, piece):
        slot = piece % 2
        S.group("pe", [lambda e, slot=slot, k=k, piece=piece: e.matmul(
            psum[6][:, 256 + 2 * piece:256 + 2 * piece + 2], lhsT=wa_st[slot][:, k, :],
            rhs=scT[:, 2 * k:2 * k + 2], start=(k == 0), stop=(k == KC - 1)) for k in range(KC)],
            reads=[b_wa[slot], b_scT], writes=[b_modps])

    def mod_piece(l, piece):
        mod_piece_dma(l, piece)
        mod_piece_mm(l, piece)

    def mod_step(l, piece):
        if piece > 0:
            mod_piece_mm(l, piece - 1)
        if piece < NPIECE:
            mod_piece_dma(l, piece)

    def mod_collect(l):
        S.op("pe", lambda e: e.transpose(psum[7][:, 0:96], stg[0:96, :], ident[0:96, 0:96]),
             reads=[b_stg, b_ident], writes=[b_ps[7]])
        S.op("dve", lambda e: e.tensor_copy(out=pcols[:, PC_G:PC_G + 24], in_=psum[7][:, 72:96]),
             reads=[b_ps[7]], writes=[b_pcols])
        S.op("dve", lambda e: e.tensor_copy(out=modb, in_=psum[7][:, 0:72]), reads=[b_ps[7]], writes=[b_modb])
        for q in range(2):
            S.op("dve", lambda e, q=q: e.tensor_tensor(out=pcols[:, PC_MOD + q:PC_MOD + 144:2], in0=psum[6][:, 256 + q:256 + 144:2],
                                                       in1=modb, op=ALU.add),
                 reads=[b_modps, b_modb], writes=[b_pcols])
        S.inherit([b_ps[6]], [b_modps])

    def mod_derive(l):
        for s in range(3):
            for q in range(2):
                sh = pcols[:, PC_MOD + 16 * (3 * s) + q: PC_MOD + 16 * (3 * s) + 16: 2]
                scl = pcols[:, PC_MOD + 16 * (3 * s + 1) + q: PC_MOD + 16 * (3 * s + 1) + 16: 2]
                gt = pcols[:, PC_MOD + 16 * (3 * s + 2) + q: PC_MOD + 16 * (3 * s + 2) + 16: 2]
                gcols = pcols[:, PC_G + 8 * s: PC_G + 8 * s + 8]
                S.op("dve", lambda e, s=s, q=q, scl=scl, gcols=gcols: e.scalar_tensor_tensor(
                    out=pcols[:, PC_GMOD + 16 * s + q: PC_GMOD + 16 * s + 16: 2], in0=scl, scalar=1.0, in1=gcols,
                    op0=ALU.add, op1=ALU.mult), reads=[b_pcols], writes=[b_pcols])
                S.op("dve", lambda e, s=s, q=q, sh=sh: e.tensor_copy(
                    out=pcols[:, PC_SHIFT + 16 * s + q: PC_SHIFT + 16 * s + 16: 2], in_=sh), reads=[b_pcols], writes=[b_pcols])
                S.op("dve", lambda e, s=s, q=q, gt=gt: e.tensor_scalar(
                    out=pcols[:, PC_COEF + 16 * s + q: PC_COEF + 16 * s + 16: 2], in0=gt,
                    scalar1=(1.0 if s == 1 else 0.5), scalar2=None, op0=ALU.mult), reads=[b_pcols], writes=[b_pcols])

    def compute_mod(l):
        mod_begin(l)
        for piece in range(NPIECE):
            mod_piece(l, piece)
        mod_collect(l)
        mod_derive(l)

    stg_cur = {"bufs": stgw, "hz": b_stgw}

    def load_cast(dst, src, n, wbuf):
        i = nxt("stgw", 2)
        sb, hz = stg_cur["bufs"][i], stg_cur["hz"][i]
        S.dma("sp", sb[:, 0:n], src, writes=[hz])
        ce = nxt("casteng", 2)
        wb = wbuf[ce] if isinstance(wbuf, list) else wbuf
        if ce == 0:
            S.op("dve", lambda e: e.tensor_copy(out=dst, in_=sb[:, 0:n]), reads=[hz], writes=[wb])
        else:
            S.op("act", lambda e: e.activation(out=dst, in_=sb[:, 0:n], func=AF.Copy), reads=[hz], writes=[wb])

    def load_ffn_gu(wg_d, wu_d, l):
        for half in range(2):
            cs = slice(half * 1408, (half + 1) * 1408)
            for (dst, src, bufs) in ((wg_sb, wg_d, b_wg), (wu_sb, wu_d, b_wu)):
                for k in range(KC):
                    load_cast(dst[:, k, cs], src[l, k * 128:(k + 1) * 128, cs], 1408, bufs[half])

    def load_ffn_d(wd_d, l):
        claim("wd", b_wd)
        for half in range(2):
            cs = slice(half * 512, (half + 1) * 512)
            for f in range(FC):
                load_cast(wd_sb[:, f, cs], wd_d[l, f * 128:(f + 1) * 128, cs], 512, b_wd2[half])

    xrot_f = [actv.view(60416 + i * 2048, [NT], F32) for i in range(2)]
    b_xrot_f = [Buf("xrotf0"), Buf("xrotf1")]
    xrot_m = [actv.view(46080 + i * 2048, [NT], F32) for i in range(2)]
    b_xrot_m = [Buf("xrotm0"), Buf("xrotm1")]
    xr = {"bufs": xrot, "hz": b_xrot}

    def use_xrot(extra, b_extra):
        xr["bufs"] = xrot + extra
        xr["hz"] = b_xrot + b_extra

    def nxt_xrot():
        n = len(xr["bufs"])
        i = nxt("xrot", 64) % n
        return xr["bufs"][i], xr["hz"][i]

    def norm_stats(t):
        for c in range(KC):
            xi, bxi = nxt_xrot()
            S.dma("sp", xi, xresT[c * 128:(c + 1) * 128, tile_cols(t)], reads=[b_xres[t][c]], writes=[bxi])
            j = nxt("sq", 2)
            S.op("act", lambda e, xi=xi, j=j: e.activation(out=sq[j], in_=xi, func=AF.Square),
                 reads=[bxi], writes=[b_sq[j]])
            S.op("pe", lambda e, j=j, c=c: e.matmul(psum[6], lhsT=ones_r, rhs=sq[j], start=(c == 0), stop=(c == KC - 1)),
                 reads=[b_sq[j], b_ones], writes=[b_ps[6]])
        S.op("act", lambda e: e.activation(out=rstd, in_=psum[6], func=AF.Ln, scale=1.0 / D, bias=col(PC_EPS, 0)),
             reads=[b_ps[6], b_pcols], writes=[b_rstd])
        S.op("act", lambda e: e.activation(out=rstd, in_=rstd, func=AF.Exp, scale=-0.5), reads=[b_rstd], writes=[b_rstd])

    def norm_apply(t, s, q, dst_fn, b_dst, chunks=None):
        for c in (range(KC) if chunks is None else chunks):
            xi, bxi = nxt_xrot()
            S.dma("sp", xi, xresT[c * 128:(c + 1) * 128, tile_cols(t)], reads=[b_xres[t][c]], writes=[bxi])
            if s < 3:
                gcol = col(PC_GMOD, 16 * s + 2 * c + q)
                S.op("dve", lambda e, xi=xi, gcol=gcol: e.scalar_tensor_tensor(
                    out=tn[0], in0=xi, scalar=gcol, in1=rstd, op0=ALU.mult, op1=ALU.mult),
                    reads=[bxi, b_rstd, b_pcols], writes=[b_tn[0]])
                shc = col(PC_SHIFT, 16 * s + 2 * c + q)
                S.op("dve", lambda e, c=c, shc=shc: e.tensor_scalar(out=dst_fn(c), in0=tn[0], scalar1=shc, scalar2=None, op0=ALU.add),
                     reads=[b_tn[0], b_pcols], writes=[b_dst])
            else:
                gcol = col(PC_FG, c)
                S.op("dve", lambda e, xi=xi, c=c, gcol=gcol: e.scalar_tensor_tensor(
                    out=dst_fn(c), in0=xi, scalar=gcol, in1=rstd, op0=ALU.mult, op1=ALU.mult),
                    reads=[bxi, b_rstd, b_pcols], writes=[b_dst])

    def res_load(t, d, xrr_l, b_xrr_l):
        i = d % len(xrr_l)
        S.dma("sp", xrr_l[i], xresT[d * 128:(d + 1) * 128, tile_cols(t)], reads=[b_xres[t][d]], writes=[b_xrr_l[i]])

    def res_apply(t, d, pd, s, q, xrr_l, b_xrr_l, xnew_l, b_xnew_l):
        i = d % len(xrr_l)
        o_ = nxt("xnew", 2)
        cf = col(PC_COEF, 16 * s + 2 * d + q)
        S.op("dve", lambda e, i=i, o_=o_, pd=pd, cf=cf: e.scalar_tensor_tensor(
            out=xnew_l[o_], in0=psum[pd], scalar=cf, in1=xrr_l[i], op0=ALU.mult, op1=ALU.add),
            reads=[b_ps[pd], b_xrr_l[i], b_pcols], writes=[b_xnew_l[o_]])
        S.dma("sp", xresT[d * 128:(d + 1) * 128, tile_cols(t)], xnew_l[o_], reads=[b_xnew_l[o_]], writes=[b_xres[t][d]])
        if d + len(xrr_l) < KC:
            res_load(t, d + len(xrr_l), xrr_l, b_xrr_l)

    def ffn_phase(l, s, pre=None, wstream=None, n_first=0, next_stream=None, side_steps=None):
        claim("mid", ffn_region_bufs + b_xrot_f)
        claim("norm", b_xrot + [b_rstd, b_tn[0], b_xnT])
        claim("wd", b_wd)
        use_xrot(xrot_f, b_xrot_f)
        if pre is not None:
            pre()
        stg_cur["bufs"], stg_cur["hz"] = stgw, b_stgw
        if wstream is not None:
            for _ in range((n_first + 1) // 2 + 1):
                wstream.pump(2)

        def prologue(t):
            q = 0 if t == 0 else 1
            norm_stats(t)
            norm_apply(t, s, q, lambda c: xnT[:, c, :], b_xnT)

        def gateup(t, j):
            pg, pu = j % 2, 2 + j % 2
            hz = 0 if j < 11 else 1
            S.group("pe", [lambda e, k=k, j=j, pg=pg: e.matmul(psum[pg], lhsT=wg_sb[:, k, j * 128:(j + 1) * 128], rhs=xnT[:, k, :],
                                                                start=(k == 0), stop=(k == KC - 1)) for k in range(KC)],
                    reads=b_wg[hz] + [b_xnT], writes=[b_ps[pg]])
            S.group("pe", [lambda e, k=k, j=j, pu=pu: e.matmul(psum[pu], lhsT=wu_sb[:, k, j * 128:(j + 1) * 128], rhs=xnT[:, k, :],
                                                                start=(k == 0), stop=(k == KC - 1)) for k in range(KC)],
                    reads=b_wu[hz] + [b_xnT], writes=[b_ps[pu]])
            gi = nxt("gsb", 2)
            S.op("act", lambda e, gi=gi, pg=pg: e.activation(out=gsb[gi], in_=psum[pg], func=AF.Silu),
                 reads=[b_ps[pg]], writes=[b_gsb[gi]])
            S.op("dve", lambda e, gi=gi, pu=pu, j=j: e.tensor_tensor(out=hT[:, j, :], in0=psum[pu], in1=gsb[gi], op=ALU.mult),
                 reads=[b_ps[pu], b_gsb[gi]], writes=[b_hT])

        def down(t, d):
            q = 0 if t == 0 else 1
            pd = 4 + d % 2
            S.group("pe", [lambda e, f=f, d=d, pd=pd: e.matmul(psum[pd], lhsT=wd_sb[:, f, d * 128:(d + 1) * 128], rhs=hT[:, f, :],
                                                                start=(f == 0), stop=(f == FC - 1)) for f in range(FC)],
                    reads=b_wd2[0 if d < 4 else 1] + [b_hT], writes=[b_ps[pd]])
            res_apply(t, d, pd, s, q, xrr, b_xrr, xnew, b_xnew)

        def prologue_apply(t):
            q = 0 if t == 0 else 1
            norm_apply(t, s, q, lambda c: xnT[:, c, :], b_xnT)

        prologue(0)
        for t in range(NTILES):
            for j in range(FC):
                gateup(t, j)
                if j == 14 and t + 1 < NTILES:
                    norm_stats(t + 1)
                if side_steps and t >= 1:
                    side_steps.pop(0)()
                if wstream is not None:
                    wstream.pump(2)
                if next_stream is not None and t == NTILES - 1 and j >= 10:
                    if j < FC - 1:
                        if next_stream.n_issued < 16:
                            next_stream.pump(2 if next_stream.n_issued + 2 <= 16 else 1)
                    else:
                        next_stream.pump(0)
            if t + 1 < NTILES:
                prologue_apply(t + 1)
            for d in range(len(xrr)):
                res_load(t, d, xrr, b_xrr)
            for d in range(KC):
                if wstream is not None:
                    wstream.pump(2)
                    wstream.pump(2)
                if next_stream is not None and t == NTILES - 1:
                    next_stream.pump(2)
                down(t, d)
        if wstream is not None:
            wstream.drain()
        while side_steps:
            side_steps.pop(0)()

    def initial_pass():
        xtok = actv.view(NORM_END, [4, D], F32)
        b_xtok = Buf("xtok")
        claim("mid", [b_xtok] + b_stgw)
        mod_begin(0)
        stg_cur["bufs"], stg_cur["hz"] = stgw, b_stgw
        for t in range(NTILES):
            if ws1_all:
                ws1_all[0].pump(2)
                ws1_all[0].pump(2)
            for p in range(8 * t, 8 * t + 8):
                mod_step(0, p)
            src = xp if t == 0 else xs[(t - 1) * NT: t * NT, :]
            S.dma("sp", xtok, src.rearrange("(b p) f -> p b f", p=128), writes=[b_xtok])
            for c in range(KC):
                bank = c % 2
                S.group("pe", [lambda e, tb=tb, c=c, bank=bank: e.transpose(psum[bank][:, tb * 128:(tb + 1) * 128],
                                                                             xtok[:, tb, c * 128:(c + 1) * 128], ident)
                               for tb in range(4)], reads=[b_xtok, b_ident], writes=[b_ps[bank]])
                i = nxt("xrot_init", 2)
                if c % 2 == 0:
                    S.op("act", lambda e, i=i, bank=bank: e.activation(out=xrot[i], in_=psum[bank], func=AF.Copy),
                         reads=[b_ps[bank]], writes=[b_xrot[i]])
                else:
                    S.op("dve", lambda e, i=i, bank=bank: e.tensor_copy(out=xrot[i], in_=psum[bank]),
                         reads=[b_ps[bank]], writes=[b_xrot[i]])
                S.dma("sp", xresT[c * 128:(c + 1) * 128, tile_cols(t)], xrot[i], reads=[b_xrot[i]], writes=[b_xres[t][c]])
        mod_step(0, NPIECE)
        mod_collect(0)
        mod_derive(0)
        if ws1_all:
            ws1_all[0].pump(0)

    def final_pass():
        xnf = [actv.view(NORM_END + i * 16384, [KC, NT], F32) for i in range(2)]
        b_xnf = [Buf("xnf0"), Buf("xnf1")]
        ytok = [actv.view(NORM_END + 32768 + i * 4096, [D], F32) for i in range(2)]
        b_ytok = [Buf("ytok0"), Buf("ytok1")]
        b_ytok_b = [Buf("ytokb0"), Buf("ytokb1")]
        claim("mid", b_xnf + b_ytok + b_ytok_b + b_xrot_f)
        claim("norm", b_xrot + [b_rstd, b_tn[0], b_xnT])
        use_xrot(xrot_f, b_xrot_f)
        n = 0

        def fnorm(t):
            norm_stats(t)
            norm_apply(t, 3, 0, lambda c, t=t: xnf[t % 2][:, c, :], b_xnf[t % 2])

        fnorm(0)
        for t in range(NTILES):
            if t + 1 < NTILES:
                fnorm(t + 1)
            xn_t, b_xn_t = xnf[t % 2], b_xnf[t % 2]
            for tb in range(4):
                yi = n % 2
                n += 1
                for half in range(2):
                    bank = half
                    S.group("pe", [lambda e, c=c, tb=tb, bank=bank, xn_t=xn_t: e.transpose(
                        psum[bank][:, (c % 4) * 128:(c % 4 + 1) * 128], xn_t[:, c, tb * 128:(tb + 1) * 128], ident)
                        for c in range(half * 4, half * 4 + 4)], reads=[b_xn_t, b_ident], writes=[b_ps[bank]])
                    if half == 0:
                        S.op("act", lambda e, yi=yi, bank=bank: e.activation(out=ytok[yi][:, 0:512], in_=psum[bank], func=AF.Copy),
                             reads=[b_ps[bank]], writes=[b_ytok[yi]])
                    else:
                        S.op("dve", lambda e, yi=yi, bank=bank: e.tensor_copy(out=ytok[yi][:, 512:1024], in_=psum[bank]),
                             reads=[b_ps[bank]], writes=[b_ytok_b[yi]])
                dst = yp[tb * 128:(tb + 1) * 128, :] if t == 0 else ys[(t - 1) * NT + tb * 128:(t - 1) * NT + (tb + 1) * 128, :]
                S.dma("sp", dst, ytok[yi], reads=[b_ytok[yi], b_ytok_b[yi]], final=True)
        S.op("pe", lambda e: e.transpose(psum[7][0:64, 0:128], nsT, ident), reads=[b_nsT, b_ident], writes=[b_ps[7]])
        S.op("dve", lambda e: e.tensor_copy(out=stg[0:64, :], in_=psum[7][0:64, 0:128]), reads=[b_ps[7]], writes=[b_stg])
        S.dma("sp", ns_out, stg[0:64, :], reads=[b_stg], final=True)
        if cfg.get("debug"):
            pc_dbg = dout("pcols_dbg", [128, 512])
            S.dma("sp", pc_dbg, pcols, reads=[b_pcols], final=True)

    MIX_OFF = NORM_END
    win_sb = actv.view(MIX_OFF, [KC, DIN], BF16)
    wout_sb = actv.view(MIX_OFF, [KC, D], BF16)
    po_sb = actv.view(MIX_OFF + 16384, [4, NT], BF16)
    rec_sb = actv.view(MIX_OFF + 20480, [4, NT], BF16)
    wpool_sb = actv.view(MIX_OFF + 24576, [4, 128], BF16)
    wgate_sb = actv.view(MIX_OFF + 25600, [16, 128], BF16)
    b_winA, b_winB = [Buf("winA0"), Buf("winA1")], [Buf("winB0"), Buf("winB1")]
    b_win2 = b_winA + b_winB
    b_wout2 = [Buf("wout0"), Buf("wout1")]
    b_wpool, b_wgate = Buf("wpool"), Buf("wgate")
    b_posb = [Buf(f"po_sb{g}") for g in range(4)]
    b_recsb = Buf("rec_sb")
    mstg = [actv.view(ACTV_B - (i + 1) * 5632, [1408], F32) for i in range(2)]
    b_mstg = [Buf("mstg0"), Buf("mstg1")]
    b_mstg_parts = [[Buf(f"mstgp{i}_{j}") for j in range(16)] for i in range(2)]
    b_upads = [Buf(f"upad{i}") for i in range(16)]
    b_urp = [[[Buf(f"urp{t}_{c}_{h}") for h in range(2)] for c in range(4)] for t in range(NTILES)]
    b_car = [Buf(f"car{j}", True) for j in range(8)]

    def zero_urec_pads():
        S.op("pool", lambda e: e.memset(stg[:, 0:4], 0.0), writes=[b_stg])
        n = 0
        for cc in range(4):
            for (c0, w) in ((0, 2), (258, 3), (517, 3), (UBASE_S + 2 + 4096, 2)):
                S.dma("sp", urecT[cc * 128:(cc + 1) * 128, c0:c0 + w], stg[:, 0:w], reads=[b_stg], writes=[b_upads[n]])
                n += 1

    def mixer_param_steps(l):
        pc = lambda base: pcols[:, base:base + 8]
        T0, T1, T2, T3 = MC_T, MC_T + 8, MC_T + 16, MC_T + 24
        rw = dict(reads=[b_pcols], writes=[b_pcols])

        def loads():
            S.dma("sp", stg2[0:16, :], conv_w[l], writes=[b_stg2])
            S.dma("sp", stg2[16:20, :], conv_b[l], writes=[b_stg2])
            S.dma("sp", stg2[20:24, :], pool_scale[l], writes=[b_stg2])
            S.dma("sp", stg2[24:32, :], lru_b_r[l], writes=[b_stg2])
            S.dma("sp", stg2[32:40, :], lru_b_i[l], writes=[b_stg2])
            S.dma("sp", stg2[40:48, :], lru_lam[l], writes=[b_stg2])
        steps = [loads, lambda: transpose_rows(48, stg2, b_stg2, pcols[:, MC:MC + 48], b_pcols)]
        ops = [
            ("dve", lambda e: e.tensor_scalar(out=pc(MC_NBR), in0=pc(MC_BR), scalar1=-1.0, scalar2=None, op0=ALU.mult)),
            ("dve", lambda e: e.tensor_scalar(out=pc(MC_NBI), in0=pc(MC_BI), scalar1=-1.0, scalar2=None, op0=ALU.mult)),
            ("dve", lambda e: e.tensor_scalar(out=pc(T3), in0=pc(MC_LAM), scalar1=-1.0, scalar2=None, op0=ALU.mult)),
            ("dve", lambda e: e.tensor_tensor(out=pc(T0), in0=pc(MC_LAM), in1=pc(T3), op=ALU.max)),
            ("act", lambda e: e.activation(out=pc(T0), in_=pc(T0), func=AF.Exp, scale=-1.0)),
            ("dve", lambda e: e.tensor_scalar(out=pc(T1), in0=pc(T0), scalar1=1.0, scalar2=None, op0=ALU.add)),
            ("dve", lambda e: e.tensor_scalar(out=pc(T2), in0=pc(T1), scalar1=-1.0, scalar2=1e-12, op0=ALU.add, op1=ALU.max)),
            ("act", lambda e: e.activation(out=pc(T1), in_=pc(T1), func=AF.Ln)),
            ("dve", lambda e: e.reciprocal(out=pc(T2), in_=pc(T2))),
            ("dve", lambda e: e.tensor_tensor(out=pc(T2), in0=pc(T2), in1=pc(T0), op=ALU.mult)),
            ("dve", lambda e: e.tensor_tensor(out=pc(T1), in0=pc(T1), in1=pc(T2), op=ALU.mult)),
            ("dve", lambda e: e.tensor_scalar(out=pc(T3), in0=pc(MC_LAM), scalar1=-1.0, scalar2=0.0, op0=ALU.mult, op1=ALU.max)),
            ("dve", lambda e: e.tensor_tensor(out=pc(T1), in0=pc(T1), in1=pc(T3), op=ALU.add)),
            ("dve", lambda e: e.tensor_scalar(out=pc(MC_HC), in0=pc(T1), scalar1=-8.0, scalar2=None, op0=ALU.mult)),
        ]
        for (eng, fn) in ops:
            steps.append(lambda eng=eng, fn=fn: S.op(eng, fn, **rw))
        return steps

    def load_mixer_params(l):
        for st_ in mixer_param_steps(l):
            st_()

    def load_mixer_weights_s1(l):
        claim("mid", b_win2 + [b_wpool, b_wgate] + b_mstg + [x for p in b_mstg_parts for x in p])
        stg_cur["bufs"], stg_cur["hz"] = mstg, b_mstg
        for k in range(KC):
            load_cast(win_sb[:, k, 0:512], w_in[l, k * 128:(k + 1) * 128, 0:512], 512, b_winA)
        for k in range(KC):
            load_cast(win_sb[:, k, 512:1536], w_in[l, k * 128:(k + 1) * 128, 512:1536], 1024, b_winB)
        i = nxt("stgw", 2)
        S.dma("sp", mstg[i][:, 0:512].rearrange("p (g d) -> p g d", g=4), w_pool[l].rearrange("g c d -> c g d"), writes=[b_mstg[i]])
        S.op("dve", lambda e, i=i: e.tensor_copy(out=wpool_sb.rearrange("p g d -> p (g d)"), in_=mstg[i][:, 0:512]),
             reads=[b_mstg[i]], writes=[b_wpool])
    def load_gate_weights(l):
        for ri, wsrc in enumerate((lru_w_r, lru_w_i)):
            i = nxt("stgw", 2)
            S.op("dve", lambda e, i=i: e.memset(mstg[i][:, 0:1024], 0.0), writes=[b_mstg[i]] + b_mstg_parts[i])
            for d in range(2):
                for cc in range(4):
                    for hh in range(2):
                        blk = (d * 4 + cc) * 128
                        pb = b_mstg_parts[i][(d * 4 + cc) * 2 + hh]
                        S.dma("sp", mstg[i][hh * 64:(hh + 1) * 64, blk + hh * 64: blk + hh * 64 + 64],
                              wsrc[l, d, 2 * cc + hh], writes=[pb])
            S.op("dve", lambda e, i=i, ri=ri: e.tensor_copy(
                out=wgate_sb[:, ri * 8:(ri + 1) * 8, :].rearrange("p a b -> p (a b)"), in_=mstg[i][:, 0:1024]),
                reads=[b_mstg[i]] + b_mstg_parts[i], writes=[b_wgate])

    def load_wout_piece(l, k):
        if k == 0:
            claim("mid", b_wout2 + b_posb + [b_recsb])
        load_cast(wout_sb[:, k, :], w_out[l, k * 128:(k + 1) * 128, :], 1024, b_wout2)

    class Streamer:
        def __init__(self):
            self.items = []
            self.pending = []
            self.n_issued = 0

        def add(self, dst, src, n, wbuf):
            self.items.append((dst, src, n, wbuf))

        def pump(self, k=2):
            for (dst, sb, hz, n, wbuf) in self.pending:
                ce = nxt("casteng", 2)
                wb = wbuf[ce] if isinstance(wbuf, list) else wbuf
                if ce == 0:
                    S.op("dve", lambda e, dst=dst, sb=sb, n=n: e.tensor_copy(out=dst, in_=sb[:, 0:n]), reads=[hz], writes=[wb])
                else:
                    S.op("act", lambda e, dst=dst, sb=sb, n=n: e.activation(out=dst, in_=sb[:, 0:n], func=AF.Copy),
                         reads=[hz], writes=[wb])
            self.pending = []
            for _ in range(min(k, 2)):
                if not self.items:
                    break
                dst, src, n, wbuf = self.items.pop(0)
                i = nxt("stgw", 2)
                sb, hz = stg_cur["bufs"][i], stg_cur["hz"][i]
                S.dma("sp", sb[:, 0:n], src, writes=[hz])
                self.pending.append((dst, sb, hz, n, wbuf))
                self.n_issued += 1

        def drain(self):
            while self.items or self.pending:
                self.pump()

    def wd_items(wd_d, l):
        items = []
        for half in range(2):
            cs = slice(half * 512, (half + 1) * 512)
            for f in range(FC):
                items.append((wd_sb[:, f, cs], wd_d[l, f * 128:(f + 1) * 128, cs], 512, b_wd2[half]))
        return items

    def gu_items(wg_d, wu_d, l):
        items = []
        for half in range(2):
            cs = slice(half * 1408, (half + 1) * 1408)
            for (dst, src, bufs) in ((wg_sb, wg_d, b_wg), (wu_sb, wu_d, b_wu)):
                for k in range(KC):
                    items.append((dst[:, k, cs], src[l, k * 128:(k + 1) * 128, cs], 1408, bufs[half]))
        return items

    def mixer_s1(l, interleave, load_weights=None):
        WD = WD_OFF
        up_hi = wbig.view(WD, [4, NT], BF16)
        up_lo = wbig.view(WD + 4096, [4, NT], BF16)
        pm_s = wbig.view(WD + 8192, [4, 128], BF16)
        pm_p = wbig.view(WD + 9216, [16, 128], BF16)
        pooled4 = [wbig.view(WD + 33792 + i * 1024, [NT], BF16) for i in range(4)]
        pob = [wbig.view(WD + 15360 + i * 1024, [NT], BF16) for i in range(2)]
        ev = [wbig.view(WD + 17408 + i * 2048, [NT], F32) for i in range(3)]
        pmst = wbig.view(WD + 23552, [20, 128], F32)
        b_uphi = [Buf(f"uphi{i}") for i in range(4)]
        b_uplo = [Buf(f"uplo{i}") for i in range(4)]
        b_pm, b_pmst = Buf("pm"), Buf("pmst")
        b_pooled4 = [Buf(f"pooled{g}") for g in range(4)]
        b_pooled = b_pooled4
        b_pob = [Buf("pob0"), Buf("pob1")]
        b_ev = [Buf(f"ev{i}") for i in range(3)]
        claim("wd", b_uphi + b_uplo + [b_pm, b_pmst] + b_pooled + b_pob + b_ev)
        claim("norm", b_xrot + [b_rstd, b_tn[0], b_xnT])
        claim("mid", b_xrot_m)
        use_xrot(xrot_m, b_xrot_m)
        S.dma("sp", pmst, poolm_in, writes=[b_pmst])
        S.op("dve", lambda e: e.tensor_copy(out=pm_s, in_=pmst[:, 0:4, :]), reads=[b_pmst], writes=[b_pm])
        S.op("dve", lambda e: e.tensor_copy(out=pm_p, in_=pmst[:, 4:20, :]), reads=[b_pmst], writes=[b_pm])

        def ucols(t):
            if t == 0:
                return [(slice(2, 258), slice(0, 256), 0), (slice(261, 517), slice(256, 512), 1)]
            b0 = UBASE_S + 2 + (t - 1) * NT
            return [(slice(b0, b0 + NT), slice(0, NT), 0)]

        norm_stats(0)
        norm_apply(0, 1, 0, lambda c: xnT[:, c, :], b_xnT)
        if load_weights is not None:
            load_weights()
        for t in range(NTILES):
            q = 0 if t == 0 else 1
            for tb in range(4):
                bank = 4 + tb % 2
                S.group("pe", [lambda e, k=k, tb=tb, bank=bank: e.matmul(psum[bank], lhsT=xnT[:, k, tb * 128:(tb + 1) * 128],
                                                                         rhs=win_sb[:, k, 0:512], start=(k == 0), stop=(k == KC - 1))
                               for k in range(KC)], reads=[b_xnT] + b_winA, writes=[b_ps[bank]])
                S.op("act", lambda e, tb=tb, bank=bank: e.activation(out=up_hi[:, tb, :], in_=psum[bank], func=AF.Copy),
                     reads=[b_ps[bank]], writes=[b_uphi[tb]])
                S.op("dve", lambda e, tb=tb, bank=bank: e.tensor_tensor(out=up_lo[:, tb, :], in0=psum[bank], in1=up_hi[:, tb, :],
                                                                        op=ALU.subtract),
                     reads=[b_ps[bank], b_uphi[tb]], writes=[b_uplo[tb]])
            for cch in range(8):
                bank = cch % 4
                S.group("pe", [lambda e, k=k, cch=cch, bank=bank: e.matmul(
                    psum[bank], lhsT=win_sb[:, k, 512 + cch * 128: 512 + (cch + 1) * 128], rhs=xnT[:, k, :],
                    start=(k == 0), stop=(k == KC - 1)) for k in range(KC)], reads=[b_xnT] + b_winB, writes=[b_ps[bank]])
                i = nxt("ev", 3)
                if cch < 4:
                    S.op("dve", lambda e, i=i, bank=bank: e.tensor_copy(out=ev[i], in_=psum[bank]),
                         reads=[b_ps[bank]], writes=[b_ev[i]])
                    for (dc, sc_, hidx) in ucols(t):
                        S.dma("sp", urecT[cch * 128:(cch + 1) * 128, dc], ev[i][:, sc_], reads=[b_ev[i]],
                              writes=[b_urp[t][cch][hidx]])
                else:
                    S.op("act", lambda e, i=i, bank=bank: e.activation(out=ev[i], in_=psum[bank], func=AF.Gelu_apprx_tanh),
                         reads=[b_ps[bank]], writes=[b_ev[i]])
                    S.dma("sp", ggT[(cch - 4) * 128:(cch - 3) * 128, tile_cols(t)], ev[i], reads=[b_ev[i]], writes=[b_gg[t][cch - 4]])

            def pooled_mm(g):
                bank = g
                fns = []
                for tb in range(4):
                    srcs = []
                    if t == 0:
                        s_, bo = tb // 2, tb % 2
                        for bi in range(2):
                            srcs.append((2 * s_ + bi, pm_p[:, g * 4 + bi * 2 + bo, :]))
                    else:
                        srcs.append((tb, pm_s[:, g, :]))
                    n = 2 * len(srcs)
                    cnt = 0
                    for (tbi, M) in srcs:
                        for up in (up_hi, up_lo):
                            fns.append(lambda e, g=g, tb=tb, tbi=tbi, M=M, up=up, bank=bank, first=(cnt == 0), last=(cnt == n - 1):
                                       e.matmul(psum[bank][:, tb * 128:(tb + 1) * 128], lhsT=up[:, tbi, g * 128:(g + 1) * 128], rhs=M,
                                                start=first, stop=last))
                            cnt += 1
                S.group("pe", fns, reads=b_uphi + b_uplo + [b_pm], writes=[b_ps[bank]])

            def pooled_evac(g):
                S.op("dve", lambda e, g=g: e.tensor_copy(out=pooled4[g], in_=psum[g]), reads=[b_ps[g]], writes=[b_pooled4[g]])

            def mixed_mm(g):
                bank = (4, 5, 7, 4)[g]
                S.op("pe", lambda e, g=g, bank=bank: e.matmul(psum[bank], lhsT=wpool_sb[:, g, :], rhs=pooled4[g], start=True, stop=True),
                     reads=[b_wpool, b_pooled4[g]], writes=[b_ps[bank]])
                oi = nxt("pob", 2)
                S.op("act", lambda e, g=g, oi=oi, bank=bank: e.activation(out=pob[oi], in_=psum[bank], func=AF.Identity,
                                                                           scale=col(MC_PS, g)),
                     reads=[b_ps[bank], b_pcols], writes=[b_pob[oi]])
                S.dma("sp", poT[g * 128:(g + 1) * 128, tile_cols(t)], pob[oi], reads=[b_pob[oi]], writes=[b_po[t][g]])

            if t + 1 < NTILES:
                norm_stats(t + 1)
            for g in range(4):
                pooled_mm(g)
            for g in range(4):
                pooled_evac(g)
            if t + 1 < NTILES:
                norm_apply(t + 1, 1, 1, lambda c: xnT[:, c, :], b_xnT)
            for g in range(4):
                mixed_mm(g)
            if interleave is not None:
                interleave(t)

    def lru_stage_a(l, t, cc, d, sl, SL, bSL, xco=None):
        xc, xcb, tr, ti, a_, a2 = (SL[k][sl] for k in ("xc", "xcb", "tr", "ti", "a", "a2"))
        bxc, bxcb, btr, bti, ba, ba2 = (bSL[k][sl] for k in ("xc", "xcb", "tr", "ti", "a", "a2"))
        if xco is not None:
            xc, bxc = xco
        j = d * 4 + cc
        S.op("dve", lambda e: e.tensor_copy(out=xcb, in_=xc), reads=[bxc], writes=[bxcb])
        pr, pi_ = (0, 1) if (cc % 2 == 0) else (2, 3)
        S.op("pe", lambda e: e.matmul(psum[pr], lhsT=wgate_sb[:, (0 * 2 + d) * 4 + cc, :], rhs=xcb, start=True, stop=True),
             reads=[b_wgate, bxcb], writes=[b_ps[pr]])
        S.op("pe", lambda e: e.matmul(psum[pi_], lhsT=wgate_sb[:, (1 * 2 + d) * 4 + cc, :], rhs=xcb, start=True, stop=True),
             reads=[b_wgate, bxcb], writes=[b_ps[pi_]])

    def lru_stage_a2(l, t, cc, d, sl, SL, bSL, xco=None):
        xc, xcb, tr, ti, a_, a2 = (SL[k][sl] for k in ("xc", "xcb", "tr", "ti", "a", "a2"))
        bxc, bxcb, btr, bti, ba, ba2 = (bSL[k][sl] for k in ("xc", "xcb", "tr", "ti", "a", "a2"))
        j = d * 4 + cc
        pr, pi_ = (0, 1) if (cc % 2 == 0) else (2, 3)
        one = col(PC_ONE, 0)
        S.op("act", lambda e: e.activation(out=tr, in_=psum[pr], func=AF.Exp, bias=col(MC_NBR, j), scale=-1.0),
             reads=[b_ps[pr], b_pcols], writes=[btr])
        S.op("act", lambda e: e.activation(out=tr, in_=tr, func=AF.Ln, bias=one, scale=1.0), reads=[btr, b_pcols], writes=[btr])
        S.op("act", lambda e: e.activation(out=tr, in_=tr, func=AF.Exp, scale=-1.0), reads=[btr], writes=[btr])
        S.op("act", lambda e: e.activation(out=a_, in_=tr, func=AF.Exp, scale=col(MC_HC, j)), reads=[btr, b_pcols], writes=[ba])
        S.op("act", lambda e: e.activation(out=a2, in_=a_, func=AF.Square), reads=[ba], writes=[ba2])
        S.op("act", lambda e: e.activation(out=ti, in_=psum[pi_], func=AF.Exp, bias=col(MC_NBI, j), scale=-1.0),
             reads=[b_ps[pi_], b_pcols], writes=[bti])
        S.op("act", lambda e: e.activation(out=ti, in_=ti, func=AF.Ln, bias=one, scale=1.0), reads=[bti, b_pcols], writes=[bti])
        S.op("act", lambda e: e.activation(out=ti, in_=ti, func=AF.Exp, scale=-1.0), reads=[bti], writes=[bti])

    def lru_stage_b(l, t, cc, d, sl, SL, bSL, xco=None):
        xc, xcb, tr, ti, a_, a2 = (SL[k][sl] for k in ("xc", "xcb", "tr", "ti", "a", "a2"))
        bxc, bxcb, btr, bti, ba, ba2 = (bSL[k][sl] for k in ("xc", "xcb", "tr", "ti", "a", "a2"))
        if xco is not None:
            xc, bxc = xco
        j = d * 4 + cc
        one = col(PC_ONE, 0)
        S.op("dve", lambda e: e.tensor_scalar(out=a2, in0=a2, scalar1=0.99999988, scalar2=None, op0=ALU.min),
             reads=[ba2], writes=[ba2])
        S.op("dve", lambda e: e.tensor_tensor(out=ti, in0=ti, in1=xc, op=ALU.mult), reads=[bti, bxc], writes=[bti])
        S.op("act", lambda e: e.activation(out=a2, in_=a2, func=AF.Ln, bias=one, scale=-1.0), reads=[ba2, b_pcols], writes=[ba2])
        S.op("act", lambda e: e.activation(out=a2, in_=a2, func=AF.Exp, scale=0.5), reads=[ba2], writes=[ba2])

    def lru_stage_b2(l, t, cc, d, sl, SL, bSL, xco=None):
        xc, xcb, tr, ti, a_, a2 = (SL[k][sl] for k in ("xc", "xcb", "tr", "ti", "a", "a2"))
        bxc, bxcb, btr, bti, ba, ba2 = (bSL[k][sl] for k in ("xc", "xcb", "tr", "ti", "a", "a2"))
        j = d * 4 + cc
        if d == 0:
            S.op("dve", lambda e: e.tensor_tensor(out=a2, in0=a2, in1=ti, op=ALU.mult), reads=[ba2, bti], writes=[ba2])
        else:
            S.op("dve", lambda e: e.tensor_tensor(out=a2[:, ::-1], in0=a2[:, ::-1], in1=ti[:, ::-1], op=ALU.mult),
                 reads=[ba2, bti], writes=[ba2])
        car = col(PC_CAR, j)
        if t == 0:
            for s_ in ((0, 1) if d == 0 else (1, 0)):
                cs = slice(s_ * 256, (s_ + 1) * 256)
                if d == 0:
                    S.op("dve", lambda e, cs=cs: e.tensor_tensor_scan(out=tr[:, cs], data0=a_[:, cs], data1=a2[:, cs], initial=0.0,
                                                                       op0=ALU.mult, op1=ALU.add), reads=[ba, ba2], writes=[btr])
                    fin = tr[:, s_ * 256 + 255: s_ * 256 + 256]
                else:
                    S.op("dve", lambda e, cs=cs: e.tensor_tensor_scan(
                        out=tr[:, cs][:, ::-1], data0=a_[:, cs][:, ::-1], data1=a2[:, cs][:, ::-1], initial=0.0,
                        op0=ALU.mult, op1=ALU.add), reads=[ba, ba2], writes=[btr])
                    fin = tr[:, s_ * 256: s_ * 256 + 1]
                nscol = ((s_ * DEPTH + l) * 2 + d) * 4 + cc
                S.op("act", lambda e, fin=fin, nscol=nscol: e.activation(out=nsT[:, nscol:nscol + 1], in_=fin, func=AF.Copy),
                     reads=[btr], writes=[b_nsT])
        else:
            first = (t == 1) if d == 0 else (t == NTILES - 1)
            init = col(PC_H0, (l * 2 + d) * 4 + cc) if first else car
            if d == 0:
                S.op("dve", lambda e, init=init: e.tensor_tensor_scan(out=tr, data0=a_, data1=a2, initial=init,
                                                                       op0=ALU.mult, op1=ALU.add),
                     reads=[ba, ba2, b_pcols, b_car[j]], writes=[btr])
                fin = tr[:, NT - 1:NT]
            else:
                S.op("dve", lambda e, init=init: e.tensor_tensor_scan(out=tr[:, ::-1], data0=a_[:, ::-1], data1=a2[:, ::-1],
                                                                       initial=init, op0=ALU.mult, op1=ALU.add),
                     reads=[ba, ba2, b_pcols, b_car[j]], writes=[btr])
                fin = tr[:, 0:1]
            S.op("act", lambda e, fin=fin, car=car: e.activation(out=car, in_=fin, func=AF.Copy), reads=[btr], writes=[b_car[j]])

    def make_slots(places, tag, with_xc=True):
        SL = {k: [] for k in ("xc", "xcb", "tr", "ti", "a", "a2")}
        bSL = {k: [] for k in SL}
        for sl, (ar, off) in enumerate(places):
            o_ = off
            if with_xc:
                SL["xc"].append(ar.view(o_, [NT], F32)); o_ += 2048
            else:
                SL["xc"].append(None)
            SL["xcb"].append(ar.view(o_, [NT], BF16)); o_ += 1024
            for k in ("tr", "ti", "a", "a2"):
                SL[k].append(ar.view(o_, [NT], F32)); o_ += 2048
            for k in SL:
                bSL[k].append(Buf(f"{tag}{k}{sl}"))
        return SL, bSL

    def slot_bufs(bSL):
        return [x for k in bSL for x in bSL[k]]

    def mixer_s2(l, interleave):
        urw = [actv.view(i * 2080, [520], F32) for i in range(6)]
        b_urw = [[Buf(f"urw{i}_{s}") for s in range(2)] for i in range(6)]
        SL, bSL = make_slots([(wbig, WD_OFF), (wbig, WD_OFF + 11264), (wbig, WD_OFF + 22528)], "s2")
        claim("norm", [x for p in b_urw for x in p])
        claim("wd", slot_bufs(bSL))

        def load_window(idx):
            t, cc = divmod(idx, 4)
            w = idx % 6
            if t == 0:
                for s_ in range(2):
                    S.dma("sp", urw[w][:, s_ * 259: s_ * 259 + 259], urecT[cc * 128:(cc + 1) * 128, s_ * 259: s_ * 259 + 259],
                          reads=[b_urp[0][cc][s_]] + b_upads, writes=[b_urw[w][s_]])
            else:
                b0 = UBASE_S + (t - 1) * NT
                rd = [b_urp[t][cc][0]] + b_upads
                if t > 1:
                    rd.append(b_urp[t - 1][cc][0])
                if t < NTILES - 1:
                    rd.append(b_urp[t + 1][cc][0])
                S.dma("sp", urw[w][:, 0:515], urecT[cc * 128:(cc + 1) * 128, b0:b0 + 515], reads=rd, writes=b_urw[w])

        def pre(t, cc, sl):
            w = (t * 4 + cc) % 6
            rds = b_urw[w]
            xc = SL["xc"][sl]
            bxc = bSL["xc"][sl]
            if t == 0:
                W = lambda k, w=w: urw[w][:, 0:518].rearrange("p (s n) -> p s n", s=2)[:, :, k:k + 256]
                XO = xc.rearrange("p (s n) -> p s n", s=2)
            else:
                W = lambda k, w=w: urw[w][:, k:k + NT]
                XO = xc
            S.op("dve", lambda e, W=W, XO=XO, cc=cc: e.tensor_scalar(
                out=XO, in0=W(0), scalar1=col(MC_CW, 0 * 4 + cc), scalar2=col(MC_CB, cc), op0=ALU.mult, op1=ALU.add),
                reads=rds + [b_pcols], writes=[bxc])
            for k in range(1, 4):
                S.op("dve", lambda e, W=W, XO=XO, cc=cc, k=k: e.scalar_tensor_tensor(
                    out=XO, in0=W(k), scalar=col(MC_CW, k * 4 + cc), in1=XO, op0=ALU.mult, op1=ALU.add),
                    reads=rds + [b_pcols, bxc], writes=[bxc])
            S.dma("sp", xcT[cc * 128:(cc + 1) * 128, tile_cols(t)], xc, reads=[bxc], writes=[b_xc[t][cc]])

        def post(t, cc, sl):
            S.dma("sp", yfT[cc * 128:(cc + 1) * 128, tile_cols(t)], SL["tr"][sl], reads=[bSL["tr"][sl]], writes=[b_yf[t][cc]])

        NPF = 3
        for idx in range(NPF):
            load_window(idx)
        NIT = 4 * NTILES
        for idx in range(NIT + 2):
            if idx < NIT:
                t, cc = divmod(idx, 4)
                if idx + NPF < NIT:
                    load_window(idx + NPF)
                pre(t, cc, idx % 3)
                lru_stage_a(l, t, cc, 0, idx % 3, SL, bSL)
            if 0 <= idx - 1 < NIT:
                pt, pcc = divmod(idx - 1, 4)
                lru_stage_b(l, pt, pcc, 0, (idx - 1) % 3, SL, bSL)
            if idx < NIT:
                lru_stage_a2(l, t, cc, 0, idx % 3, SL, bSL)
            if 0 <= idx - 2 < NIT:
                qt, qcc = divmod(idx - 2, 4)
                lru_stage_b2(l, qt, qcc, 0, (idx - 2) % 3, SL, bSL)
                post(qt, qcc, (idx - 2) % 3)
            if idx < NIT and interleave is not None:
                interleave(idx)

    def mixer_s3(l, interleave=None):
        WD = WD_OFF
        SL, bSL = make_slots([(wbig, WD), (wbig, WD + 9216), (wbig, WD + 18432)], "s3", with_xc=False)
        xrr3 = [wbig.view(WD + 27648 + i * 2048, [NT], F32) for i in range(2)] + [actv.view(62464, [NT], F32)]
        xnew3 = [wbig.view(WD + 31744 + i * 2048, [NT], F32) for i in range(2)]
        NS = 4
        yfb = [actv.view(i * 2048, [NT], F32) for i in range(NS)]
        xcl = [actv.view(8192 + i * 2048, [NT], F32) for i in range(NS)]
        ggb = [actv.view(54272 + i * 2048, [NT], F32) for i in range(NS)]
        po2 = [po_sb, actv.view(MIX_OFF + 29696, [4, NT], BF16)]
        rec2 = [rec_sb, actv.view(MIX_OFF + 29696 + 4096, [4, NT], BF16)]
        b_yfb = [Buf(f"yfb{i}") for i in range(NS)]
        b_ggb = [Buf(f"ggb{i}") for i in range(NS)]
        b_xcl = [Buf(f"xcl{i}") for i in range(NS)]
        b_xrr3 = [Buf("xrr3_0"), Buf("xrr3_1"), Buf("xrr3_2")]
        b_xnew3 = [Buf("xnew3_0"), Buf("xnew3_1")]
        b_po2 = [b_posb, [Buf(f"po2_{g}") for g in range(4)]]
        b_rec2 = [b_recsb, Buf("rec2")]
        claim("wd", slot_bufs(bSL) + b_xrr3[:2] + b_xnew3)
        claim("norm", b_yfb + b_xcl)
        claim("mid", b_po2[1] + [b_rec2[1]] + b_ggb + b_xrr3[2:])
        items = [(t, cc) for t in range(NTILES - 1, -1, -1) for cc in range(4)]

        def loads(idx):
            t, cc = items[idx]
            k = idx % NS
            S.dma("sp", xcl[k], xcT[cc * 128:(cc + 1) * 128, tile_cols(t)], reads=[b_xc[t][cc]], writes=[b_xcl[k]])
            S.dma("sp", yfb[k], yfT[cc * 128:(cc + 1) * 128, tile_cols(t)], reads=[b_yf[t][cc]], writes=[b_yfb[k]])
            S.dma("sp", ggb[k], ggT[cc * 128:(cc + 1) * 128, tile_cols(t)], reads=[b_gg[t][cc]], writes=[b_ggb[k]])


        def post(idx, sl):
            t, cc = items[idx]
            k = idx % NS
            par = (idx // 4) % 2
            tr, btr = SL["tr"][sl], bSL["tr"][sl]
            ysum, bys = SL["ti"][sl], bSL["ti"][sl]
            S.op("dve", lambda e: e.tensor_tensor(out=ysum[:, ::-1], in0=tr[:, ::-1], in1=yfb[k][:, ::-1], op=ALU.add),
                 reads=[btr, b_yfb[k]], writes=[bys])
            S.op("dve", lambda e: e.tensor_tensor(out=rec2[par][:, cc, :][:, ::-1], in0=ysum[:, ::-1], in1=ggb[k][:, ::-1], op=ALU.mult),
                 reads=[bys, b_ggb[k]], writes=[b_rec2[par]])

        wq, wdone = [], []

        def wout_enqueue(t, par):
            wq.extend((t, par, d) for d in range(KC))

        def wout_step(n_new=cfg.get("wout_n", 2)):
            for (t, par, d) in wdone:
                res_apply(t, d, 4 + d % 2, 1, 0 if t == 0 else 1, xrr3, b_xrr3, xnew3, b_xnew3)
            wdone[:] = []
            for _ in range(n_new):
                if not wq:
                    break
                t, par, d = wq.pop(0)
                pd = 4 + d % 2
                if d < len(xrr3):
                    res_load(t, d, xrr3, b_xrr3)
                S.group("pe", [lambda e, k=k, d=d, pd=pd, par=par: e.matmul(
                    psum[pd], lhsT=wout_sb[:, k, d * 128:(d + 1) * 128],
                    rhs=(po2[par][:, k, :] if k < 4 else rec2[par][:, k - 4, :]),
                    start=(k == 0), stop=(k == KC - 1)) for k in range(KC)],
                    reads=b_wout2 + [b_rec2[par]] + b_po2[par], writes=[b_ps[pd]])
                wdone.append((t, par, d))

        NPF = 2
        for idx in range(NPF):
            loads(idx)
        NIT = len(items)
        xo = lambda i: (xcl[i % NS], b_xcl[i % NS])
        for idx in range(NIT + 2):
            if idx < NIT:
                t, cc = items[idx]
                lru_stage_a(l, t, cc, 1, idx % 3, SL, bSL, xco=xo(idx))
            if 0 <= idx - 1 < NIT:
                pt, pcc = items[idx - 1]
                lru_stage_b(l, pt, pcc, 1, (idx - 1) % 3, SL, bSL, xco=xo(idx - 1))
            if idx < NIT:
                lru_stage_a2(l, t, cc, 1, idx % 3, SL, bSL, xco=xo(idx))
            if 0 <= idx - 2 < NIT:
                qt, qcc = items[idx - 2]
                lru_stage_b2(l, qt, qcc, 1, (idx - 2) % 3, SL, bSL, xco=xo(idx - 2))
                post(idx - 2, (idx - 2) % 3)
            if idx + NPF < NIT:
                loads(idx + NPF)
            wout_step()
            if 0 <= idx - 2 < NIT and items[idx - 2][1] == 3:
                wout_enqueue(items[idx - 2][0], ((idx - 2) // 4) % 2)
            if 0 <= idx - 2 < NIT and items[idx - 2][1] == 0:
                tq = items[idx - 2][0]
                par = ((idx - 2) // 4) % 2
                for g in range(4):
                    S.dma("sp", po2[par][:, g, :], poT[g * 128:(g + 1) * 128, tile_cols(tq)], reads=[b_po[tq][g]],
                          writes=[b_po2[par][g]])
            if idx < NIT and interleave is not None:
                interleave(idx)
        while wq or wdone:
            wout_step()

    ws1_all = []
    for l in range(depth):
        w_ = Streamer()
        for it in gu_items(f1g, f1u, l) + wd_items(f1d, l):
            w_.add(*it)
        ws1_all.append(w_)
    initial_pass()
    zero_urec_pads()
    for l in range(depth):
        if do_ffn:
            stg_cur["bufs"], stg_cur["hz"] = stgw, b_stgw

            ws1 = ws1_all[l]
            ffn_phase(l, 0, wstream=ws1, n_first=max(0, 16 - ws1.n_issued))
        ws = Streamer()
        if do_ffn:
            for it in gu_items(f2g, f2u, l):
                ws.add(*it)
        if do_mixer:
            load_mixer_params(l)
            def il1(t, ws=ws, l=l):
                ws.pump(2)
                if t == 1:
                    ws.pump(0)
                    load_gate_weights(l)
            mixer_s1(l, il1, load_weights=lambda l=l: load_mixer_weights_s1(l))
            nxt_mod = (l + 1 < depth)
            if nxt_mod:
                mod_begin(l + 1)
            wo = [(wout_sb[:, k, :], w_out[l, k * 128:(k + 1) * 128, :], 1024, b_wout2) for k in range(KC)]
            ws.items[0:0] = wo
            first_s2 = [True]

            def il2(idx, ws=ws, l=l, nxt_mod=nxt_mod, first_s2=first_s2):
                if first_s2[0]:
                    claim("mid", b_wout2 + b_posb + [b_recsb])
                    first_s2[0] = False
                ws.pump(1)
                if nxt_mod:
                    mod_step(l + 1, idx)
            mixer_s2(l, il2)
            ws.drain()

            def il3(idx, l=l, nxt_mod=nxt_mod):
                if nxt_mod:
                    mod_step(l + 1, 36 + idx)
            mixer_s3(l, il3)
            if nxt_mod:
                mod_step(l + 1, NPIECE)
                mod_collect(l + 1)
        if do_ffn:
            ws.drain()
            ws2 = Streamer()
            for it in wd_items(f2d, l):
                ws2.add(*it)
            ffn_phase(l, 2, wstream=ws2, n_first=0, next_stream=(ws1_all[l + 1] if l + 1 < depth else None))
        if l + 1 < depth:
            if do_mixer:
                mod_derive(l + 1)
            else:
                compute_mod(l + 1)
    final_pass()

    with nc.Block() as block:
        S.emit(block)
    prog = {"nc": nc, "stack": stack, "S": S}
    return prog


def _pool_matrices():
    wins = (2, 4, 8, 16)
    out = np.zeros((20, 128, 128), np.float32)

    def full(L, w):
        t = np.arange(L)
        lo = np.clip(t - w // 2, 0, L)
        hi = np.clip(t + w // 2, 0, L)
        M = np.zeros((L, L), np.float64)
        for to in range(L):
            M[lo[to]:hi[to], to] = 1.0 / (hi[to] - lo[to])
            M[to, to] -= 1.0
        return M
    for g, w in enumerate(wins):
        M64 = full(64, w)
        out[g, 0:64, 0:64] = M64
        out[g, 64:128, 64:128] = M64
        M256 = full(256, w)
        for bi in range(2):
            for bo in range(2):
                out[4 + g * 4 + bi * 2 + bo] = M256[bi * 128:(bi + 1) * 128, bo * 128:(bo + 1) * 128]
    return np.ascontiguousarray(out.transpose(1, 0, 2))


_PROG_CACHE = {}


def _make_in_maps(inputs):
    f = lambda a: np.ascontiguousarray(np.asarray(a, dtype=np.float32))
    x_prompt = f(inputs["x_prompt"]); x_sample = f(inputs["x_sample"])
    state_lru = f(inputs["state_lru"]); c = f(inputs["c"]); c_ctx = f(inputs["c_ctx"])
    shared = {
        "eye": np.eye(128, dtype=np.float32),
        "poolm": _pool_matrices(),
        "norm_g": f(inputs["norm_g"]).reshape(DEPTH, 24, 128),
        "w_ada": f(inputs["w_ada"]),
        "b_ada": f(inputs["b_ada"]).reshape(DEPTH, 72, 128),
        "ffn1_gate": f(inputs["ffn1_gate"]), "ffn1_up": f(inputs["ffn1_up"]), "ffn1_down": f(inputs["ffn1_down"]),
        "w_in": f(inputs["w_in"]),
        "conv_w": f(inputs["conv_w"]).reshape(DEPTH, 16, 128),
        "conv_b": f(inputs["conv_b"]).reshape(DEPTH, 4, 128),
        "w_pool": f(inputs["w_pool"]),
        "pool_scale": f(inputs["pool_scale"]).reshape(DEPTH, 4, 128),
        "lru_w_r": f(inputs["lru_w_r"]), "lru_b_r": f(inputs["lru_b_r"]).reshape(DEPTH, 8, 128),
        "lru_w_i": f(inputs["lru_w_i"]), "lru_b_i": f(inputs["lru_b_i"]).reshape(DEPTH, 8, 128),
        "lru_lambda": f(inputs["lru_lambda"]).reshape(DEPTH, 8, 128),
        "w_out": f(inputs["w_out"]),
        "ffn2_gate": f(inputs["ffn2_gate"]), "ffn2_up": f(inputs["ffn2_up"]), "ffn2_down": f(inputs["ffn2_down"]),
        "final_g": f(inputs["final_g"]).reshape(8, 128),
    }
    in_maps = []
    for i in range(N_CORES):
        cc = np.empty((16, 128), np.float32)
        cc[0::2] = c_ctx.reshape(8, 128)
        cc[1::2] = c[i].reshape(8, 128)
        m = dict(shared)
        m["xp"] = np.ascontiguousarray(x_prompt[2 * i:2 * i + 2].reshape(512, D))
        m["xs"] = np.ascontiguousarray(x_sample[i])
        m["st"] = np.ascontiguousarray(state_lru[i].reshape(32, 128))
        m["cc"] = cc
        in_maps.append(m)
    return in_maps


def kernel(**inputs):
    if "prog" not in _PROG_CACHE:
        _PROG_CACHE["prog"] = build_program()
    prog = _PROG_CACHE["prog"]
    in_maps = _make_in_maps(inputs)
    res = run_bass_kernel_spmd(prog["nc"], in_maps, core_ids=list(range(N_CORES)))
    r = res.results
    y_prompt = np.stack([r[i]["yp"].reshape(2, 256, D) for i in range(N_CORES)], 0).reshape(16, 256, D)
    y_sample = np.stack([r[i]["ys"] for i in range(N_CORES)], 0)
    ns = np.stack([r[i]["ns"].reshape(2, DEPTH, 2, DLRU) for i in range(N_CORES)], 0).reshape(16, DEPTH, 2, DLRU)
    return (y_prompt.astype(np.float32), y_sample.astype(np.float32), ns.astype(np.float32))
```

```python
import contextlib
import numpy as np
import concourse.bass as bass
import concourse.mybir as mybir
from concourse.bass_utils import run_bass_kernel_spmd

F32 = mybir.dt.float32
BF16 = mybir.dt.bfloat16
F32R = mybir.dt.float32r
AF = mybir.ActivationFunctionType
ALU = mybir.AluOpType

D = 1024
KC = 8
DFF = 2816
FC = 22
DEPTH = 4
NT = 512
NTILES = 9
TTOK = NT * NTILES
DIN = 1536
DLRU = 512
EPS = 1e-6
N_CORES = 8
LP = 4620
UBASE_S = 518


class Buf:
    __slots__ = ("name", "w", "r", "strict")

    def __init__(self, name, strict=False):
        self.name = name
        self.w = None
        self.r = {}
        self.strict = strict


class _Eng:
    def __init__(self, name):
        self.name = name
        self.sem = None
        self.count = 0
        self.seen = {}
        self.ops = []
        self.dma_sems = []
        self.dma_vals = []
        self.dma_rr = 0


class Sched:
    ENGS = ("pe", "act", "dve", "pool", "sp")

    def __init__(self, nc, stack, n_dma_sems=None):
        self.nc = nc
        self.e = {n: _Eng(n) for n in self.ENGS}
        n_dma_sems = n_dma_sems or {"sp": 16, "pool": 10, "act": 4}
        for n in self.ENGS:
            self.e[n].sem = stack.enter_context(nc.semaphore("s_" + n))
        for q, k in n_dma_sems.items():
            for i in range(k):
                self.e[q].dma_sems.append(stack.enter_context(nc.semaphore(f"d_{q}{i}")))
                self.e[q].dma_vals.append(0)
        self.final_tokens = []
        self.n_ops = 0

    @staticmethod
    def _key(sem):
        return id(sem)

    def _collect(self, eng, reads, writes, drop_own):
        deps = {}
        def add(tok, strict):
            if tok is None:
                return
            sem, val = tok
            k = id(sem)
            if drop_own and not strict and sem is eng.sem:
                return
            if k not in deps or deps[k][1] < val:
                deps[k] = (sem, val)
        for b in reads:
            add(b.w, b.strict)
        for b in writes:
            add(b.w, b.strict)
            for t in b.r.values():
                add(t, b.strict)
        waits = []
        for k, (sem, val) in deps.items():
            if eng.seen.get(k, 0) >= val:
                continue
            eng.seen[k] = val
            waits.append((sem, val))
        return waits

    def _commit(self, tok, reads, writes):
        k = id(tok[0])
        for b in reads:
            b.r[k] = tok
        for b in writes:
            b.w = tok
            b.r = {}

    def op(self, engine, fn, reads=(), writes=(), strict=False):
        return self.group(engine, [fn], reads, writes, strict=strict)

    @staticmethod
    def inherit(new_bufs, old_bufs):
        toks = {}
        for b in old_bufs:
            for t in ([b.w] if b.w is not None else []) + list(b.r.values()):
                k = id(t[0])
                if k not in toks or toks[k][1] < t[1]:
                    toks[k] = t
        for nb in new_bufs:
            for k, t in toks.items():
                if k not in nb.r or nb.r[k][1] < t[1]:
                    nb.r[k] = t

    def group(self, engine, fns, reads=(), writes=(), strict=False):
        eng = self.e[engine]
        waits = self._collect(eng, reads, writes, drop_own=not strict)
        eng.count += 1
        tok = (eng.sem, eng.count)
        self._commit(tok, reads, writes)
        for i, fn in enumerate(fns):
            eng.ops.append((waits if i == 0 else (), fn, (eng.sem, 1) if i == len(fns) - 1 else None))
        self.n_ops += len(fns)
        return tok

    def dma(self, queue, out, in_, reads=(), writes=(), final=False, **kw):
        eng = self.e[queue]
        waits = self._collect(eng, reads, writes, drop_own=False)
        i = eng.dma_rr
        eng.dma_rr = (i + 1) % len(eng.dma_sems)
        sem = eng.dma_sems[i]
        prev = eng.dma_vals[i]
        k = id(sem)
        if prev > 0 and eng.seen.get(k, 0) < prev:
            eng.seen[k] = prev
            waits.append((sem, prev))
        val = prev + 16
        eng.dma_vals[i] = val
        tok = (sem, val)
        self._commit(tok, reads, writes)
        eng.ops.append((waits, (lambda e, out=out, in_=in_, kw=kw: e.dma_start(out=out, in_=in_, **kw)), (sem, 16)))
        self.n_ops += 1
        if final:
            self.final_tokens.append(tok)
        return tok

    def emit(self, block):
        sp = self.e["sp"]
        fin = []
        for sem, val in self.final_tokens:
            if sp.seen.get(id(sem), 0) < val:
                sp.seen[id(sem)] = val
                fin.append((sem, val))
        for n in ("pe", "act", "dve", "pool"):
            en = self.e[n]
            if en.count > 0:
                fin.append((en.sem, en.count))

        def run(eng_state, extra_waits=()):
            def body(e):
                for waits, fn, inc in eng_state.ops:
                    for sem, val in waits:
                        e.wait_ge(sem, val)
                    ins = fn(e)
                    if inc is not None:
                        ins.then_inc(inc[0], inc[1])
                for sem, val in extra_waits:
                    e.wait_ge(sem, val)
            return body

        block.tensor(run(self.e["pe"]))
        block.scalar(run(self.e["act"]))
        block.vector(run(self.e["dve"]))
        block.gpsimd(run(self.e["pool"]))
        block.sync(run(sp, fin))


class Arena:
    def __init__(self, nc, stack, name, nbytes):
        self.nbytes = nbytes
        self.t = stack.enter_context(nc.sbuf_tensor(name, [128, nbytes // 4], F32))

    def view(self, off, shape_free, dt):
        esz = {F32: 4, F32R: 4, BF16: 2}[dt]
        n = int(np.prod(shape_free))
        assert off % 4 == 0 and off + n * esz <= self.nbytes, (off, n * esz, self.nbytes)
        w = (n * esz + 3) // 4
        ap = self.t[:, off // 4: off // 4 + w]
        if dt != F32:
            ap = ap.bitcast(dt)
        if len(shape_free) == 1:
            return ap
        names = " ".join(f"a{i}" for i in range(len(shape_free)))
        kw = {f"a{i}": int(s) for i, s in enumerate(shape_free)}
        return ap.rearrange(f"p ({names}) -> p {names}", **kw)


def build_program(cfg=None):
    cfg = cfg or {}
    depth = cfg.get("depth", DEPTH)
    do_mixer = cfg.get("mixer", True)
    do_ffn = cfg.get("ffn", True)

    nc = bass.Bass("TRN2", target_bir_lowering=False)
    stack = contextlib.ExitStack()

    def din(name, shape, dt=F32):
        return nc.dram_tensor(name, list(shape), dt, kind="ExternalInput").ap()

    def dout(name, shape, dt=F32):
        return nc.dram_tensor(name, list(shape), dt, kind="ExternalOutput").ap()

    def dscr(name, shape, dt=F32):
        return nc.dram_tensor(name, list(shape), dt, kind=("ExternalOutput" if cfg.get("debug") else "Internal")).ap()

    xp = din("xp", [512, D])
    xs = din("xs", [4096, D])
    st_in = din("st", [32, 128])
    cc_in = din("cc", [16, 128])
    eye_in = din("eye", [128, 128])
    poolm_in = din("poolm", [128, 20, 128])
    norm_g = din("norm_g", [DEPTH, 24, 128])
    w_ada = din("w_ada", [DEPTH, D, 9 * D])
    b_ada = din("b_ada", [DEPTH, 72, 128])
    f1g = din("ffn1_gate", [DEPTH, D, DFF])
    f1u = din("ffn1_up", [DEPTH, D, DFF])
    f1d = din("ffn1_down", [DEPTH, DFF, D])
    w_in = din("w_in", [DEPTH, D, DIN])
    conv_w = din("conv_w", [DEPTH, 16, 128])
    conv_b = din("conv_b", [DEPTH, 4, 128])
    w_pool = din("w_pool", [DEPTH, 4, 128, 128])
    pool_scale = din("pool_scale", [DEPTH, 4, 128])
    lru_w_r = din("lru_w_r", [DEPTH, 2, 8, 64, 64])
    lru_b_r = din("lru_b_r", [DEPTH, 8, 128])
    lru_w_i = din("lru_w_i", [DEPTH, 2, 8, 64, 64])
    lru_b_i = din("lru_b_i", [DEPTH, 8, 128])
    lru_lam = din("lru_lambda", [DEPTH, 8, 128])
    w_out = din("w_out", [DEPTH, D, D])
    f2g = din("ffn2_gate", [DEPTH, D, DFF])
    f2u = din("ffn2_up", [DEPTH, D, DFF])
    f2d = din("ffn2_down", [DEPTH, DFF, D])
    final_g = din("final_g", [8, 128])
    yp = dout("yp", [512, D])
    ys = dout("ys", [4096, D])
    ns_out = dout("ns", [64, 128])
    xresT = dscr("xresT", [D, TTOK])
    urecT = dscr("urecT", [DLRU, LP])
    xcT = dscr("xcT", [DLRU, TTOK])
    yfT = dscr("yfT", [DLRU, TTOK])
    ggT = dscr("ggT", [DLRU, TTOK])
    poT = dscr("poT", [DLRU, TTOK], BF16)

    S = Sched(nc, stack)

    WBIG_B = 3 * 45056
    ACTV_B = 64512
    wbig = Arena(nc, stack, "wbig", WBIG_B)
    actv = Arena(nc, stack, "actv", ACTV_B)
    const = Arena(nc, stack, "const", 8192)
    ones_t = stack.enter_context(nc.sbuf_tensor("ones_r", [128, 128], F32R))
    ones_r = ones_t[:]
    sq_t = stack.enter_context(nc.sbuf_tensor("sq_r", [128, 2, NT], F32R))

    WD_OFF = 90112
    wg_sb = wbig.view(0, [KC, DFF], BF16)
    wu_sb = wbig.view(45056, [KC, DFF], BF16)
    wd_sb = wbig.view(WD_OFF, [FC, D], BF16)
    b_wg = [[Buf("wgA0"), Buf("wgA1")], [Buf("wgB0"), Buf("wgB1")]]
    b_wu = [[Buf("wuA0"), Buf("wuA1")], [Buf("wuB0"), Buf("wuB1")]]
    b_wd2 = [[Buf("wdL0"), Buf("wdL1")], [Buf("wdR0"), Buf("wdR1")]]
    b_wd = [x for h in b_wd2 for x in h]

    ident = const.view(0, [128], F32)
    pcols = const.view(512, [512], F32)
    stg = const.view(2560, [128], F32)
    stg2 = const.view(3072, [128], F32)
    scT = const.view(3584, [16], F32)
    nhalf = const.view(3648, [NT], F32)
    phalf = const.view(5696, [NT], F32)
    b_ident, b_ones, b_pcols, b_scT = Buf("ident"), Buf("ones"), Buf("pcols", True), Buf("scT", True)
    b_stg, b_stg2, b_half = Buf("stg", True), Buf("stg2", True), Buf("half")

    PC_MOD, PC_G, PC_GMOD, PC_SHIFT, PC_COEF, PC_FG = 0, 144, 168, 216, 264, 312
    MC = 320
    MC_CW, MC_CB, MC_PS, MC_BR, MC_BI, MC_LAM = MC + 0, MC + 16, MC + 20, MC + 24, MC + 32, MC + 40
    MC_NBR, MC_NBI, MC_HC = MC + 48, MC + 56, MC + 64
    MC_T = MC + 72
    PC_H0 = 424
    PC_NS = 456
    PC_CAR = 456
    PC_EPS = 464
    PC_ONE = 465

    def col(base, idx):
        return pcols[:, base + idx: base + idx + 1]

    psum = [stack.enter_context(nc.psum_tensor(f"ps{i}", [128, 512], F32))[:] for i in range(8)]
    b_ps = [Buf(f"ps{i}") for i in range(8)]
    b_modps = Buf("modps", True)

    o = 0
    xrot = [actv.view(o + i * 2048, [NT], F32) for i in range(2)]; o += 4096
    rstd = actv.view(o, [NT], F32); o += 2048
    tn = [actv.view(o, [NT], F32)]; o += 2048
    xnT = actv.view(o, [KC, NT], BF16); o += 8192
    NORM_END = o
    b_xrot = [Buf(f"xrot{i}") for i in range(2)]
    sq = [sq_t[:, i, :] for i in range(2)]
    b_sq = [Buf(f"sq{i}") for i in range(2)]
    b_rstd, b_tn, b_xnT = Buf("rstd"), [Buf("tn0")], Buf("xnT")
    o = NORM_END
    hT = actv.view(o, [FC, NT], BF16); o += 22528
    gsb = [actv.view(o + i * 1024, [NT], BF16) for i in range(2)]; o += 2048
    xrr = [actv.view(o + i * 2048, [NT], F32) for i in range(2)]; o += 4096
    xnew = [actv.view(o + i * 2048, [NT], F32) for i in range(2)]; o += 4096
    XRR_OFF = o - 8192
    stgw = [actv.view(o + i * 5632, [1408], F32) for i in range(2)]; o += 11264
    STGW_OFF = o - 11264
    assert o <= ACTV_B, o
    b_hT = Buf("hT")
    b_gsb = [Buf(f"gsb{i}") for i in range(2)]
    b_xrr = [Buf(f"xrr{i}") for i in range(2)]
    b_xnew = [Buf(f"xnew{i}") for i in range(2)]
    b_stgw = [Buf(f"stgw{i}") for i in range(2)]
    ffn_region_bufs = [b_hT] + b_gsb + b_xrr + b_xnew + b_stgw

    b_xres = [[Buf(f"xres{t}_{c}") for c in range(KC)] for t in range(NTILES)]
    b_urec = [[Buf(f"urec{t}_{c}") for c in range(4)] for t in range(NTILES)]
    b_xc = [[Buf(f"xc{t}_{c}") for c in range(4)] for t in range(NTILES)]
    b_yf = [[Buf(f"yf{t}_{c}") for c in range(4)] for t in range(NTILES)]
    b_gg = [[Buf(f"gg{t}_{c}") for c in range(4)] for t in range(NTILES)]
    b_po = [[Buf(f"po{t}_{c}") for c in range(4)] for t in range(NTILES)]
    b_upad = Buf("upad")

    rr = {}

    def nxt(name, n):
        v = rr.get(name, 0)
        rr[name] = (v + 1) % n
        return v

    def tile_cols(t):
        return slice(t * NT, (t + 1) * NT)

    region_all = {"norm": [], "mid": [], "wd": []}

    def claim(region, bufs):
        cur = region_all[region]
        ids = {id(x) for x in bufs}
        S.inherit(bufs, [x for x in cur if id(x) not in ids])
        have = {id(x) for x in cur}
        cur.extend(x for x in bufs if id(x) not in have)

    S.dma("sp", ident, eye_in, writes=[b_ident])
    S.op("pool", lambda e: e.memset(tn[0][:, 0:128], 1.0), writes=[b_tn[0]])
    S.op("dve", lambda e: e.tensor_copy(out=ones_r, in_=tn[0][:, 0:128]), reads=[b_tn[0]], writes=[b_ones])
    S.op("pool", lambda e: e.memset(nhalf, -0.5), writes=[b_half])
    S.op("pool", lambda e: e.memset(phalf, 0.5), writes=[b_half])
    S.op("pool", lambda e: e.memset(pcols, 0.0), writes=[b_pcols])
    S.op("pool", lambda e: e.memset(pcols[:, PC_EPS:PC_EPS + 1], EPS), writes=[b_pcols])
    S.op("pool", lambda e: e.memset(pcols[:, PC_ONE:PC_ONE + 1], 1.0), writes=[b_pcols])

    def transpose_rows(nrows, src_stage, b_src, dst_cols, b_dst, bank=7):
        S.op("pe", lambda e: e.transpose(psum[bank][:, 0:nrows], src_stage[0:nrows, :], ident[0:nrows, 0:nrows]),
             reads=[b_src, b_ident], writes=[b_ps[bank]])
        S.op("dve", lambda e: e.tensor_copy(out=dst_cols, in_=psum[bank][:, 0:nrows]),
             reads=[b_ps[bank]], writes=[b_dst])

    S.dma("sp", stg[0:16, :], cc_in, writes=[b_stg])
    S.op("act", lambda e: e.activation(out=stg[0:16, :], in_=stg[0:16, :], func=AF.Silu), reads=[b_stg], writes=[b_stg])
    transpose_rows(16, stg, b_stg, scT, b_scT)
    S.dma("sp", stg2[0:8, :], final_g, writes=[b_stg2])
    S.dma("sp", stg2[32:64, :], st_in, writes=[b_stg2])
    S.op("pe", lambda e: e.transpose(psum[7][:, 0:64], stg2[0:64, :], ident[0:64, 0:64]),
         reads=[b_stg2, b_ident], writes=[b_ps[7]])
    S.op("dve", lambda e: e.tensor_copy(out=pcols[:, PC_FG:PC_FG + 8], in_=psum[7][:, 0:8]), reads=[b_ps[7]], writes=[b_pcols])
    S.op("dve", lambda e: e.tensor_copy(out=pcols[:, PC_H0:PC_H0 + 32], in_=psum[7][:, 32:64]), reads=[b_ps[7]], writes=[b_pcols])

    nsT = const.view(7744, [64], F32)
    b_nsT = Buf("nsT", True)
    modb = stg2[:, 0:72]
    b_modb = b_stg2
    S.op("pool", lambda e: e.memset(nsT, 0.0), writes=[b_nsT])

    wa_st = [wbig.view(WD_OFF + 36864 + i * 4096, [KC, 128], F32) for i in range(2)]
    b_wa = [Buf("wa0"), Buf("wa1")]
    wa_all = list(b_wa)
    NPIECE = 72

    def mod_begin(l):
        claim("wd", wa_all)
        S.inherit([b_modps], [b_ps[6]])
        S.dma("sp", stg[0:72, :], b_ada[l], writes=[b_stg])
        S.dma("sp", stg[72:96, :], norm_g[l], writes=[b_stg])

    def mod_piece_dma(l, piece):
        slot = piece % 2
        S.dma("sp", wa_st[slot], w_ada[l, :, piece * 128:(piece + 1) * 128].rearrange("(k p) c -> p k c", p=128),
              writes=[b_wa[slot]])

    def mod_piece_mm(l, piece):
        slot = piece % 2
        S.group("pe", [lambda e, slot=slot, k=k, piece=piece: e.matmul(
            psum[6][:, 256 + 2 * piece:256 + 2 * piece + 2], lhsT=wa_st[slot][:, k, :],
            rhs=scT[:, 2 * k:2 * k + 2], start=(k == 0), stop=(k == KC - 1)) for k in range(KC)],
            reads=[b_wa[slot], b_scT], writes=[b_modps])

    def mod_piece(l, piece):
        mod_piece_dma(l, piece)
        mod_piece_mm(l, piece)

    def mod_step(l, piece):
        if piece > 0:
            mod_piece_mm(l, piece - 1)
        if piece < NPIECE:
            mod_piece_dma(l, piece)

    def mod_collect(l):
        S.op("pe", lambda e: e.transpose(psum[7][:, 0:96], stg[0:96, :], ident[0:96, 0:96]),
             reads=[b_stg, b_ident], writes=[b_ps[7]])
        S.op("dve", lambda e: e.tensor_copy(out=pcols[:, PC_G:PC_G + 24], in_=psum[7][:, 72:96]),
             reads=[b_ps[7]], writes=[b_pcols])
        S.op("dve", lambda e: e.tensor_copy(out=modb, in_=psum[7][:, 0:72]), reads=[b_ps[7]], writes=[b_modb])
        for q in range(2):
            S.op("dve", lambda e, q=q: e.tensor_tensor(out=pcols[:, PC_MOD + q:PC_MOD + 144:2], in0=psum[6][:, 256 + q:256 + 144:2],
                                                       in1=modb, op=ALU.add),
                 reads=[b_modps, b_modb], writes=[b_pcols])
        S.inherit([b_ps[6]], [b_modps])

    def mod_derive(l):
        for s in range(3):
            for q in range(2):
                sh = pcols[:, PC_MOD + 16 * (3 * s) + q: PC_MOD + 16 * (3 * s) + 16: 2]
                scl = pcols[:, PC_MOD + 16 * (3 * s + 1) + q: PC_MOD + 16 * (3 * s + 1) + 16: 2]
                gt = pcols[:, PC_MOD + 16 * (3 * s + 2) + q: PC_MOD + 16 * (3 * s + 2) + 16: 2]
                gcols = pcols[:, PC_G + 8 * s: PC_G + 8 * s + 8]
                S.op("dve", lambda e, s=s, q=q, scl=scl, gcols=gcols: e.scalar_tensor_tensor(
                    out=pcols[:, PC_GMOD + 16 * s + q: PC_GMOD + 16 * s + 16: 2], in0=scl, scalar=1.0, in1=gcols,
                    op0=ALU.add, op1=ALU.mult), reads=[b_pcols], writes=[b_pcols])
                S.op("dve", lambda e, s=s, q=q, sh=sh: e.tensor_copy(
                    out=pcols[:, PC_SHIFT + 16 * s + q: PC_SHIFT + 16 * s + 16: 2], in_=sh), reads=[b_pcols], writes=[b_pcols])
                S.op("dve", lambda e, s=s, q=q, gt=gt: e.tensor_scalar(
                    out=pcols[:, PC_COEF + 16 * s + q: PC_COEF + 16 * s + 16: 2], in0=gt,
                    scalar1=(1.0 if s == 1 else 0.5), scalar2=None, op0=ALU.mult), reads=[b_pcols], writes=[b_pcols])

    def compute_mod(l):
        mod_begin(l)
        for piece in range(NPIECE):
            mod_piece(l, piece)
        mod_collect(l)
        mod_derive(l)

    stg_cur = {"bufs": stgw, "hz": b_stgw}

    def load_cast(dst, src, n, wbuf):
        i = nxt("stgw", 2)
        sb, hz = stg_cur["bufs"][i], stg_cur["hz"][i]
        S.dma("sp", sb[:, 0:n], src, writes=[hz])
        ce = nxt("casteng", 2)
        wb = wbuf[ce] if isinstance(wbuf, list) else wbuf
        if ce == 0:
            S.op("dve", lambda e: e.tensor_copy(out=dst, in_=sb[:, 0:n]), reads=[hz], writes=[wb])
        else:
            S.op("act", lambda e: e.activation(out=dst, in_=sb[:, 0:n], func=AF.Copy), reads=[hz], writes=[wb])

    def load_ffn_gu(wg_d, wu_d, l):
        for half in range(2):
            cs = slice(half * 1408, (half + 1) * 1408)
            for (dst, src, bufs) in ((wg_sb, wg_d, b_wg), (wu_sb, wu_d, b_wu)):
                for k in range(KC):
                    load_cast(dst[:, k, cs], src[l, k * 128:(k + 1) * 128, cs], 1408, bufs[half])

    def load_ffn_d(wd_d, l):
        claim("wd", b_wd)
        for half in range(2):
            cs = slice(half * 512, (half + 1) * 512)
            for f in range(FC):
                load_cast(wd_sb[:, f, cs], wd_d[l, f * 128:(f + 1) * 128, cs], 512, b_wd2[half])

    xrot_f = [actv.view(60416 + i * 2048, [NT], F32) for i in range(2)]
    b_xrot_f = [Buf("xrotf0"), Buf("xrotf1")]
    xrot_m = [actv.view(46080 + i * 2048, [NT], F32) for i in range(2)]
    b_xrot_m = [Buf("xrotm0"), Buf("xrotm1")]
    xr = {"bufs": xrot, "hz": b_xrot}

    def use_xrot(extra, b_extra):
        xr["bufs"] = xrot + extra
        xr["hz"] = b_xrot + b_extra

    def nxt_xrot():
        n = len(xr["bufs"])
        i = nxt("xrot", 64) % n
        return xr["bufs"][i], xr["hz"][i]

    def norm_stats(t):
        for c in range(KC):
            xi, bxi = nxt_xrot()
            S.dma("sp", xi, xresT[c * 128:(c + 1) * 128, tile_cols(t)], reads=[b_xres[t][c]], writes=[bxi])
            j = nxt("sq", 2)
            S.op("act", lambda e, xi=xi, j=j: e.activation(out=sq[j], in_=xi, func=AF.Square),
                 reads=[bxi], writes=[b_sq[j]])
            S.op("pe", lambda e, j=j, c=c: e.matmul(psum[6], lhsT=ones_r, rhs=sq[j], start=(c == 0), stop=(c == KC - 1)),
                 reads=[b_sq[j], b_ones], writes=[b_ps[6]])
        S.op("act", lambda e: e.activation(out=rstd, in_=psum[6], func=AF.Ln, scale=1.0 / D, bias=col(PC_EPS, 0)),
             reads=[b_ps[6], b_pcols], writes=[b_rstd])
        S.op("act", lambda e: e.activation(out=rstd, in_=rstd, func=AF.Exp, scale=-0.5), reads=[b_rstd], writes=[b_rstd])

    def norm_apply(t, s, q, dst_fn, b_dst, chunks=None):
        for c in (range(KC) if chunks is None else chunks):
            xi, bxi = nxt_xrot()
            S.dma("sp", xi, xresT[c * 128:(c + 1) * 128, tile_cols(t)], reads=[b_xres[t][c]], writes=[bxi])
            if s < 3:
                gcol = col(PC_GMOD, 16 * s + 2 * c + q)
                S.op("dve", lambda e, xi=xi, gcol=gcol: e.scalar_tensor_tensor(
                    out=tn[0], in0=xi, scalar=gcol, in1=rstd, op0=ALU.mult, op1=ALU.mult),
                    reads=[bxi, b_rstd, b_pcols], writes=[b_tn[0]])
                shc = col(PC_SHIFT, 16 * s + 2 * c + q)
                S.op("dve", lambda e, c=c, shc=shc: e.tensor_scalar(out=dst_fn(c), in0=tn[0], scalar1=shc, scalar2=None, op0=ALU.add),
                     reads=[b_tn[0], b_pcols], writes=[b_dst])
            else:
                gcol = col(PC_FG, c)
                S.op("dve", lambda e, xi=xi, c=c, gcol=gcol: e.scalar_tensor_tensor(
                    out=dst_fn(c), in0=xi, scalar=gcol, in1=rstd, op0=ALU.mult, op1=ALU.mult),
                    reads=[bxi, b_rstd, b_pcols], writes=[b_dst])

    def res_load(t, d, xrr_l, b_xrr_l):
        i = d % len(xrr_l)
        S.dma("sp", xrr_l[i], xresT[d * 128:(d + 1) * 128, tile_cols(t)], reads=[b_xres[t][d]], writes=[b_xrr_l[i]])

    def res_apply(t, d, pd, s, q, xrr_l, b_xrr_l, xnew_l, b_xnew_l):
        i = d % len(xrr_l)
        o_ = nxt("xnew", 2)
        cf = col(PC_COEF, 16 * s + 2 * d + q)
        S.op("dve", lambda e, i=i, o_=o_, pd=pd, cf=cf: e.scalar_tensor_tensor(
            out=xnew_l[o_], in0=psum[pd], scalar=cf, in1=xrr_l[i], op0=ALU.mult, op1=ALU.add),
            reads=[b_ps[pd], b_xrr_l[i], b_pcols], writes=[b_xnew_l[o_]])
        S.dma("sp", xresT[d * 128:(d + 1) * 128, tile_cols(t)], xnew_l[o_], reads=[b_xnew_l[o_]], writes=[b_xres[t][d]])
        if d + len(xrr_l) < KC:
            res_load(t, d + len(xrr_l), xrr_l, b_xrr_l)

    def ffn_phase(l, s, pre=None, wstream=None, n_first=0, next_stream=None, side_steps=None):
        claim("mid", ffn_region_bufs + b_xrot_f)
        claim("norm", b_xrot + [b_rstd, b_tn[0], b_xnT])
        claim("wd", b_wd)
        use_xrot(xrot_f, b_xrot_f)
        if pre is not None:
            pre()
        stg_cur["bufs"], stg_cur["hz"] = stgw, b_stgw
        if wstream is not None:
            for _ in range((n_first + 1) // 2 + 1):
                wstream.pump(2)

        def prologue(t):
            q = 0 if t == 0 else 1
            norm_stats(t)
            norm_apply(t, s, q, lambda c: xnT[:, c, :], b_xnT)

        def gateup(t, j):
            pg, pu = j % 2, 2 + j % 2
            hz = 0 if j < 11 else 1
            S.group("pe", [lambda e, k=k, j=j, pg=pg: e.matmul(psum[pg], lhsT=wg_sb[:, k, j * 128:(j + 1) * 128], rhs=xnT[:, k, :],
                                                                start=(k == 0), stop=(k == KC - 1)) for k in range(KC)],
                    reads=b_wg[hz] + [b_xnT], writes=[b_ps[pg]])
            S.group("pe", [lambda e, k=k, j=j, pu=pu: e.matmul(psum[pu], lhsT=wu_sb[:, k, j * 128:(j + 1) * 128], rhs=xnT[:, k, :],
                                                                start=(k == 0), stop=(k == KC - 1)) for k in range(KC)],
                    reads=b_wu[hz] + [b_xnT], writes=[b_ps[pu]])
            gi = nxt("gsb", 2)
            S.op("act", lambda e, gi=gi, pg=pg: e.activation(out=gsb[gi], in_=psum[pg], func=AF.Silu),
                 reads=[b_ps[pg]], writes=[b_gsb[gi]])
            S.op("dve", lambda e, gi=gi, pu=pu, j=j: e.tensor_tensor(out=hT[:, j, :], in0=psum[pu], in1=gsb[gi], op=ALU.mult),
                 reads=[b_ps[pu], b_gsb[gi]], writes=[b_hT])

        def down(t, d):
            q = 0 if t == 0 else 1
            pd = 4 + d % 2
            S.group("pe", [lambda e, f=f, d=d, pd=pd: e.matmul(psum[pd], lhsT=wd_sb[:, f, d * 128:(d + 1) * 128], rhs=hT[:, f, :],
                                                                start=(f == 0), stop=(f == FC - 1)) for f in range(FC)],
                    reads=b_wd2[0 if d < 4 else 1] + [b_hT], writes=[b_ps[pd]])
            res_apply(t, d, pd, s, q, xrr, b_xrr, xnew, b_xnew)

        def prologue_apply(t):
            q = 0 if t == 0 else 1
            norm_apply(t, s, q, lambda c: xnT[:, c, :], b_xnT)

        prologue(0)
        for t in range(NTILES):
            for j in range(FC):
                gateup(t, j)
                if j == 14 and t + 1 < NTILES:
                    norm_stats(t + 1)
                if side_steps and t >= 1:
                    side_steps.pop(0)()
                if wstream is not None:
                    wstream.pump(2)
                if next_stream is not None and t == NTILES - 1 and j >= 10:
                    if j < FC - 1:
                        if next_stream.n_issued < 16:
                            next_stream.pump(2 if next_stream.n_issued + 2 <= 16 else 1)
                    else:
                        next_stream.pump(0)
            if t + 1 < NTILES:
                prologue_apply(t + 1)
            for d in range(len(xrr)):
                res_load(t, d, xrr, b_xrr)
            for d in range(KC):
                if wstream is not None:
                    wstream.pump(2)
                    wstream.pump(2)
                if next_stream is not None and t == NTILES - 1:
                    next_stream.pump(2)
                down(t, d)
        if wstream is not None:
            wstream.drain()
        while side_steps:
            side_steps.pop(0)()

    def initial_pass():
        xtok = actv.view(NORM_END, [4, D], F32)
        b_xtok = Buf("xtok")
        claim("mid", [b_xtok] + b_stgw)
        mod_begin(0)
        stg_cur["bufs"], stg_cur["hz"] = stgw, b_stgw
        for t in range(NTILES):
            if ws1_all:
                ws1_all[0].pump(2)
                ws1_all[0].pump(2)
            for p in range(8 * t, 8 * t + 8):
                mod_step(0, p)
            src = xp if t == 0 else xs[(t - 1) * NT: t * NT, :]
            S.dma("sp", xtok, src.rearrange("(b p) f -> p b f", p=128), writes=[b_xtok])
            for c in range(KC):
                bank = c % 2
                S.group("pe", [lambda e, tb=tb, c=c, bank=bank: e.transpose(psum[bank][:, tb * 128:(tb + 1) * 128],
                                                                             xtok[:, tb, c * 128:(c + 1) * 128], ident)
                               for tb in range(4)], reads=[b_xtok, b_ident], writes=[b_ps[bank]])
                i = nxt("xrot_init", 2)
                if c % 2 == 0:
                    S.op("act", lambda e, i=i, bank=bank: e.activation(out=xrot[i], in_=psum[bank], func=AF.Copy),
                         reads=[b_ps[bank]], writes=[b_xrot[i]])
                else:
                    S.op("dve", lambda e, i=i, bank=bank: e.tensor_copy(out=xrot[i], in_=psum[bank]),
                         reads=[b_ps[bank]], writes=[b_xrot[i]])
                S.dma("sp", xresT[c * 128:(c + 1) * 128, tile_cols(t)], xrot[i], reads=[b_xrot[i]], writes=[b_xres[t][c]])
        mod_step(0, NPIECE)
        mod_collect(0)
        mod_derive(0)
        if ws1_all:
            ws1_all[0].pump(0)

    def final_pass():
        xnf = [actv.view(NORM_END + i * 16384, [KC, NT], F32) for i in range(2)]
        b_xnf = [Buf("xnf0"), Buf("xnf1")]
        ytok = [actv.view(NORM_END + 32768 + i * 4096, [D], F32) for i in range(2)]
        b_ytok = [Buf("ytok0"), Buf("ytok1")]
        b_ytok_b = [Buf("ytokb0"), Buf("ytokb1")]
        claim("mid", b_xnf + b_ytok + b_ytok_b + b_xrot_f)
        claim("norm", b_xrot + [b_rstd, b_tn[0], b_xnT])
        use_xrot(xrot_f, b_xrot_f)
        n = 0

        def fnorm(t):
            norm_stats(t)
            norm_apply(t, 3, 0, lambda c, t=t: xnf[t % 2][:, c, :], b_xnf[t % 2])

        fnorm(0)
        for t in range(NTILES):
            if t + 1 < NTILES:
                fnorm(t + 1)
            xn_t, b_xn_t = xnf[t % 2], b_xnf[t % 2]
            for tb in range(4):
                yi = n % 2
                n += 1
                for half in range(2):
                    bank = half
                    S.group("pe", [lambda e, c=c, tb=tb, bank=bank, xn_t=xn_t: e.transpose(
                        psum[bank][:, (c % 4) * 128:(c % 4 + 1) * 128], xn_t[:, c, tb * 128:(tb + 1) * 128], ident)
                        for c in range(half * 4, half * 4 + 4)], reads=[b_xn_t, b_ident], writes=[b_ps[bank]])
                    if half == 0:
                        S.op("act", lambda e, yi=yi, bank=bank: e.activation(out=ytok[yi][:, 0:512], in_=psum[bank], func=AF.Copy),
                             reads=[b_ps[bank]], writes=[b_ytok[yi]])
                    else:
                        S.op("dve", lambda e, yi=yi, bank=bank: e.tensor_copy(out=ytok[yi][:, 512:1024], in_=psum[bank]),
                             reads=[b_ps[bank]], writes=[b_ytok_b[yi]])
                dst = yp[tb * 128:(tb + 1) * 128, :] if t == 0 else ys[(t - 1) * NT + tb * 128:(t - 1) * NT + (tb + 1) * 128, :]
                S.dma("sp", dst, ytok[yi], reads=[b_ytok[yi], b_ytok_b[yi]], final=True)
        S.op("pe", lambda e: e.transpose(psum[7][0:64, 0:128], nsT, ident), reads=[b_nsT, b_ident], writes=[b_ps[7]])
        S.op("dve", lambda e: e.tensor_copy(out=stg[0:64, :], in_=psum[7][0:64, 0:128]), reads=[b_ps[7]], writes=[b_stg])
        S.dma("sp", ns_out, stg[0:64, :], reads=[b_stg], final=True)
        if cfg.get("debug"):
            pc_dbg = dout("pcols_dbg", [128, 512])
            S.dma("sp", pc_dbg, pcols, reads=[b_pcols], final=True)

    MIX_OFF = NORM_END
    win_sb = actv.view(MIX_OFF, [KC, DIN], BF16)
    wout_sb = actv.view(MIX_OFF, [KC, D], BF16)
    po_sb = actv.view(MIX_OFF + 16384, [4, NT], BF16)
    rec_sb = actv.view(MIX_OFF + 20480, [4, NT], BF16)
    wpool_sb = actv.view(MIX_OFF + 24576, [4, 128], BF16)
    wgate_sb = actv.view(MIX_OFF + 25600, [16, 128], BF16)
    b_winA, b_winB = [Buf("winA0"), Buf("winA1")], [Buf("winB0"), Buf("winB1")]
    b_win2 = b_winA + b_winB
    b_wout2 = [Buf("wout0"), Buf("wout1")]
    b_wpool, b_wgate = Buf("wpool"), Buf("wgate")
    b_posb = [Buf(f"po_sb{g}") for g in range(4)]
    b_recsb = Buf("rec_sb")
    mstg = [actv.view(ACTV_B - (i + 1) * 5632, [1408], F32) for i in range(2)]
    b_mstg = [Buf("mstg0"), Buf("mstg1")]
    b_mstg_parts = [[Buf(f"mstgp{i}_{j}") for j in range(16)] for i in range(2)]
    b_upads = [Buf(f"upad{i}") for i in range(16)]
    b_urp = [[[Buf(f"urp{t}_{c}_{h}") for h in range(2)] for c in range(4)] for t in range(NTILES)]
    b_car = [Buf(f"car{j}", True) for j in range(8)]

    def zero_urec_pads():
        S.op("pool", lambda e: e.memset(stg[:, 0:4], 0.0), writes=[b_stg])
        n = 0
        for cc in range(4):
            for (c0, w) in ((0, 2), (258, 3), (517, 3), (UBASE_S + 2 + 4096, 2)):
                S.dma("sp", urecT[cc * 128:(cc + 1) * 128, c0:c0 + w], stg[:, 0:w], reads=[b_stg], writes=[b_upads[n]])
                n += 1

    def mixer_param_steps(l):
        pc = lambda base: pcols[:, base:base + 8]
        T0, T1, T2, T3 = MC_T, MC_T + 8, MC_T + 16, MC_T + 24
        rw = dict(reads=[b_pcols], writes=[b_pcols])

        def loads():
            S.dma("sp", stg2[0:16, :], conv_w[l], writes=[b_stg2])
            S.dma("sp", stg2[16:20, :], conv_b[l], writes=[b_stg2])
            S.dma("sp", stg2[20:24, :], pool_scale[l], writes=[b_stg2])
            S.dma("sp", stg2[24:32, :], lru_b_r[l], writes=[b_stg2])
            S.dma("sp", stg2[32:40, :], lru_b_i[l], writes=[b_stg2])
            S.dma("sp", stg2[40:48, :], lru_lam[l], writes=[b_stg2])
        steps = [loads, lambda: transpose_rows(48, stg2, b_stg2, pcols[:, MC:MC + 48], b_pcols)]
        ops = [
            ("dve", lambda e: e.tensor_scalar(out=pc(MC_NBR), in0=pc(MC_BR), scalar1=-1.0, scalar2=None, op0=ALU.mult)),
            ("dve", lambda e: e.tensor_scalar(out=pc(MC_NBI), in0=pc(MC_BI), scalar1=-1.0, scalar2=None, op0=ALU.mult)),
            ("dve", lambda e: e.tensor_scalar(out=pc(T3), in0=pc(MC_LAM), scalar1=-1.0, scalar2=None, op0=ALU.mult)),
            ("dve", lambda e: e.tensor_tensor(out=pc(T0), in0=pc(MC_LAM), in1=pc(T3), op=ALU.max)),
            ("act", lambda e: e.activation(out=pc(T0), in_=pc(T0), func=AF.Exp, scale=-1.0)),
            ("dve", lambda e: e.tensor_scalar(out=pc(T1), in0=pc(T0), scalar1=1.0, scalar2=None, op0=ALU.add)),
            ("dve", lambda e: e.tensor_scalar(out=pc(T2), in0=pc(T1), scalar1=-1.0, scalar2=1e-12, op0=ALU.add, op1=ALU.max)),
            ("act", lambda e: e.activation(out=pc(T1), in_=pc(T1), func=AF.Ln)),
            ("dve", lambda e: e.reciprocal(out=pc(T2), in_=pc(T2))),
            ("dve", lambda e: e.tensor_tensor(out=pc(T2), in0=pc(T2), in1=pc(T0), op=ALU.mult)),
            ("dve", lambda e: e.tensor_tensor(out=pc(T1), in0=pc(T1), in1=pc(T2), op=ALU.mult)),
            ("dve", lambda e: e.tensor_scalar(out=pc(T3), in0=pc(MC_LAM), scalar1=-1.0, scalar2=0.0, op0=ALU.mult, op1=ALU.max)),
            ("dve", lambda e: e.tensor_tensor(out=pc(T1), in0=pc(T1), in1=pc(T3), op=ALU.add)),
            ("dve", lambda e: e.tensor_scalar(out=pc(MC_HC), in0=pc(T1), scalar1=-8.0, scalar2=None, op0=ALU.mult)),
        ]
        for (eng, fn) in ops:
            steps.append(lambda eng=eng, fn=fn: S.op(eng, fn, **rw))
        return steps

    def load_mixer_params(l):
        for st_ in mixer_param_steps(l):
            st_()

    def load_mixer_weights_s1(l):
        claim("mid", b_win2 + [b_wpool, b_wgate] + b_mstg + [x for p in b_mstg_parts for x in p])
        stg_cur["bufs"], stg_cur["hz"] = mstg, b_mstg
        for k in range(KC):
            load_cast(win_sb[:, k, 0:512], w_in[l, k * 128:(k + 1) * 128, 0:512], 512, b_winA)
        for k in range(KC):
            load_cast(win_sb[:, k, 512:1536], w_in[l, k * 128:(k + 1) * 128, 512:1536], 1024, b_winB)
        i = nxt("stgw", 2)
        S.dma("sp", mstg[i][:, 0:512].rearrange("p (g d) -> p g d", g=4), w_pool[l].rearrange("g c d -> c g d"), writes=[b_mstg[i]])
        S.op("dve", lambda e, i=i: e.tensor_copy(out=wpool_sb.rearrange("p g d -> p (g d)"), in_=mstg[i][:, 0:512]),
             reads=[b_mstg[i]], writes=[b_wpool])
    def load_gate_weights(l):
        for ri, wsrc in enumerate((lru_w_r, lru_w_i)):
            i = nxt("stgw", 2)
            S.op("dve", lambda e, i=i: e.memset(mstg[i][:, 0:1024], 0.0), writes=[b_mstg[i]] + b_mstg_parts[i])
            for d in range(2):
                for cc in range(4):
                    for hh in range(2):
                        blk = (d * 4 + cc) * 128
                        pb = b_mstg_parts[i][(d * 4 + cc) * 2 + hh]
                        S.dma("sp", mstg[i][hh * 64:(hh + 1) * 64, blk + hh * 64: blk + hh * 64 + 64],
                              wsrc[l, d, 2 * cc + hh], writes=[pb])
            S.op("dve", lambda e, i=i, ri=ri: e.tensor_copy(
                out=wgate_sb[:, ri * 8:(ri + 1) * 8, :].rearrange("p a b -> p (a b)"), in_=mstg[i][:, 0:1024]),
                reads=[b_mstg[i]] + b_mstg_parts[i], writes=[b_wgate])

    def load_wout_piece(l, k):
        if k == 0:
            claim("mid", b_wout2 + b_posb + [b_recsb])
        load_cast(wout_sb[:, k, :], w_out[l, k * 128:(k + 1) * 128, :], 1024, b_wout2)

    class Streamer:
        def __init__(self):
            self.items = []
            self.pending = []
            self.n_issued = 0

        def add(self, dst, src, n, wbuf):
            self.items.append((dst, src, n, wbuf))

        def pump(self, k=2):
            for (dst, sb, hz, n, wbuf) in self.pending:
                ce = nxt("casteng", 2)
                wb = wbuf[ce] if isinstance(wbuf, list) else wbuf
                if ce == 0:
                    S.op("dve", lambda e, dst=dst, sb=sb, n=n: e.tensor_copy(out=dst, in_=sb[:, 0:n]), reads=[hz], writes=[wb])
                else:
                    S.op("act", lambda e, dst=dst, sb=sb, n=n: e.activation(out=dst, in_=sb[:, 0:n], func=AF.Copy),
                         reads=[hz], writes=[wb])
            self.pending = []
            for _ in range(min(k, 2)):
                if not self.items:
                    break
                dst, src, n, wbuf = self.items.pop(0)
                i = nxt("stgw", 2)
                sb, hz = stg_cur["bufs"][i], stg_cur["hz"][i]
                S.dma("sp", sb[:, 0:n], src, writes=[hz])
                self.pending.append((dst, sb, hz, n, wbuf))
                self.n_issued += 1

        def drain(self):
            while self.items or self.pending:
                self.pump()

    def wd_items(wd_d, l):
        items = []
        for half in range(2):
            cs = slice(half * 512, (half + 1) * 512)
            for f in range(FC):
                items.append((wd_sb[:, f, cs], wd_d[l, f * 128:(f + 1) * 128, cs], 512, b_wd2[half]))
        return items

    def gu_items(wg_d, wu_d, l):
        items = []
        for half in range(2):
            cs = slice(half * 1408, (half + 1) * 1408)
            for (dst, src, bufs) in ((wg_sb, wg_d, b_wg), (wu_sb, wu_d, b_wu)):
                for k in range(KC):
                    items.append((dst[:, k, cs], src[l, k * 128:(k + 1) * 128, cs], 1408, bufs[half]))
        return items

    def mixer_s1(l, interleave, load_weights=None):
        WD = WD_OFF
        up_hi = wbig.view(WD, [4, NT], BF16)
        up_lo = wbig.view(WD + 4096, [4, NT], BF16)
        pm_s = wbig.view(WD + 8192, [4, 128], BF16)
        pm_p = wbig.view(WD + 9216, [16, 128], BF16)
        pooled4 = [wbig.view(WD + 33792 + i * 1024, [NT], BF16) for i in range(4)]
        pob = [wbig.view(WD + 15360 + i * 1024, [NT], BF16) for i in range(2)]
        ev = [wbig.view(WD + 17408 + i * 2048, [NT], F32) for i in range(3)]
        pmst = wbig.view(WD + 23552, [20, 128], F32)
        b_uphi = [Buf(f"uphi{i}") for i in range(4)]
        b_uplo = [Buf(f"uplo{i}") for i in range(4)]
        b_pm, b_pmst = Buf("pm"), Buf("pmst")
        b_pooled4 = [Buf(f"pooled{g}") for g in range(4)]
        b_pooled = b_pooled4
        b_pob = [Buf("pob0"), Buf("pob1")]
        b_ev = [Buf(f"ev{i}") for i in range(3)]
        claim("wd", b_uphi + b_uplo + [b_pm, b_pmst] + b_pooled + b_pob + b_ev)
        claim("norm", b_xrot + [b_rstd, b_tn[0], b_xnT])
        claim("mid", b_xrot_m)
        use_xrot(xrot_m, b_xrot_m)
        S.dma("sp", pmst, poolm_in, writes=[b_pmst])
        S.op("dve", lambda e: e.tensor_copy(out=pm_s, in_=pmst[:, 0:4, :]), reads=[b_pmst], writes=[b_pm])
        S.op("dve", lambda e: e.tensor_copy(out=pm_p, in_=pmst[:, 4:20, :]), reads=[b_pmst], writes=[b_pm])

        def ucols(t):
            if t == 0:
                return [(slice(2, 258), slice(0, 256), 0), (slice(261, 517), slice(256, 512), 1)]
            b0 = UBASE_S + 2 + (t - 1) * NT
            return [(slice(b0, b0 + NT), slice(0, NT), 0)]

        norm_stats(0)
        norm_apply(0, 1, 0, lambda c: xnT[:, c, :], b_xnT)
        if load_weights is not None:
            load_weights()
        for t in range(NTILES):
            q = 0 if t == 0 else 1
            for tb in range(4):
                bank = 4 + tb % 2
                S.group("pe", [lambda e, k=k, tb=tb, bank=bank: e.matmul(psum[bank], lhsT=xnT[:, k, tb * 128:(tb + 1) * 128],
                                                                         rhs=win_sb[:, k, 0:512], start=(k == 0), stop=(k == KC - 1))
                               for k in range(KC)], reads=[b_xnT] + b_winA, writes=[b_ps[bank]])
                S.op("act", lambda e, tb=tb, bank=bank: e.activation(out=up_hi[:, tb, :], in_=psum[bank], func=AF.Copy),
                     reads=[b_ps[bank]], writes=[b_uphi[tb]])
                S.op("dve", lambda e, tb=tb, bank=bank: e.tensor_tensor(out=up_lo[:, tb, :], in0=psum[bank], in1=up_hi[:, tb, :],
                                                                        op=ALU.subtract),
                     reads=[b_ps[bank], b_uphi[tb]], writes=[b_uplo[tb]])
            for cch in range(8):
                bank = cch % 4
                S.group("pe", [lambda e, k=k, cch=cch, bank=bank: e.matmul(
                    psum[bank], lhsT=win_sb[:, k, 512 + cch * 128: 512 + (cch + 1) * 128], rhs=xnT[:, k, :],
                    start=(k == 0), stop=(k == KC - 1)) for k in range(KC)], reads=[b_xnT] + b_winB, writes=[b_ps[bank]])
                i = nxt("ev", 3)
                if cch < 4:
                    S.op("dve", lambda e, i=i, bank=bank: e.tensor_copy(out=ev[i], in_=psum[bank]),
                         reads=[b_ps[bank]], writes=[b_ev[i]])
                    for (dc, sc_, hidx) in ucols(t):
                        S.dma("sp", urecT[cch * 128:(cch + 1) * 128, dc], ev[i][:, sc_], reads=[b_ev[i]],
                              writes=[b_urp[t][cch][hidx]])
                else:
                    S.op("act", lambda e, i=i, bank=bank: e.activation(out=ev[i], in_=psum[bank], func=AF.Gelu_apprx_tanh),
                         reads=[b_ps[bank]], writes=[b_ev[i]])
                    S.dma("sp", ggT[(cch - 4) * 128:(cch - 3) * 128, tile_cols(t)], ev[i], reads=[b_ev[i]], writes=[b_gg[t][cch - 4]])

            def pooled_mm(g):
                bank = g
                fns = []
                for tb in range(4):
                    srcs = []
                    if t == 0:
                        s_, bo = tb // 2, tb % 2
                        for bi in range(2):
                            srcs.append((2 * s_ + bi, pm_p[:, g * 4 + bi * 2 + bo, :]))
                    else:
                        srcs.append((tb, pm_s[:, g, :]))
                    n = 2 * len(srcs)
                    cnt = 0
                    for (tbi, M) in srcs:
                        for up in (up_hi, up_lo):
                            fns.append(lambda e, g=g, tb=tb, tbi=tbi, M=M, up=up, bank=bank, first=(cnt == 0), last=(cnt == n - 1):
                                       e.matmul(psum[bank][:, tb * 128:(tb + 1) * 128], lhsT=up[:, tbi, g * 128:(g + 1) * 128], rhs=M,
                                                start=first, stop=last))
                            cnt += 1
                S.group("pe", fns, reads=b_uphi + b_uplo + [b_pm], writes=[b_ps[bank]])

            def pooled_evac(g):
                S.op("dve", lambda e, g=g: e.tensor_copy(out=pooled4[g], in_=psum[g]), reads=[b_ps[g]], writes=[b_pooled4[g]])

            def mixed_mm(g):
                bank = (4, 5, 7, 4)[g]
                S.op("pe", lambda e, g=g, bank=bank: e.matmul(psum[bank], lhsT=wpool_sb[:, g, :], rhs=pooled4[g], start=True, stop=True),
                     reads=[b_wpool, b_pooled4[g]], writes=[b_ps[bank]])
                oi = nxt("pob", 2)
                S.op("act", lambda e, g=g, oi=oi, bank=bank: e.activation(out=pob[oi], in_=psum[bank], func=AF.Identity,
                                                                           scale=col(MC_PS, g)),
                     reads=[b_ps[bank], b_pcols], writes=[b_pob[oi]])
                S.dma("sp", poT[g * 128:(g + 1) * 128, tile_cols(t)], pob[oi], reads=[b_pob[oi]], writes=[b_po[t][g]])

            if t + 1 < NTILES:
                norm_stats(t + 1)
            for g in range(4):
                pooled_mm(g)
            for g in range(4):
                pooled_evac(g)
            if t + 1 < NTILES:
                norm_apply(t + 1, 1, 1, lambda c: xnT[:, c, :], b_xnT)
            for g in range(4):
                mixed_mm(g)
            if interleave is not None:
                interleave(t)

    def lru_stage_a(l, t, cc, d, sl, SL, bSL, xco=None):
        xc, xcb, tr, ti, a_, a2 = (SL[k][sl] for k in ("xc", "xcb", "tr", "ti", "a", "a2"))
        bxc, bxcb, btr, bti, ba, ba2 = (bSL[k][sl] for k in ("xc", "xcb", "tr", "ti", "a", "a2"))
        if xco is not None:
            xc, bxc = xco
        j = d * 4 + cc
        S.op("dve", lambda e: e.tensor_copy(out=xcb, in_=xc), reads=[bxc], writes=[bxcb])
        pr, pi_ = (0, 1) if (cc % 2 == 0) else (2, 3)
        S.op("pe", lambda e: e.matmul(psum[pr], lhsT=wgate_sb[:, (0 * 2 + d) * 4 + cc, :], rhs=xcb, start=True, stop=True),
             reads=[b_wgate, bxcb], writes=[b_ps[pr]])
        S.op("pe", lambda e: e.matmul(psum[pi_], lhsT=wgate_sb[:, (1 * 2 + d) * 4 + cc, :], rhs=xcb, start=True, stop=True),
             reads=[b_wgate, bxcb], writes=[b_ps[pi_]])

    def lru_stage_a2(l, t, cc, d, sl, SL, bSL, xco=None):
        xc, xcb, tr, ti, a_, a2 = (SL[k][sl] for k in ("xc", "xcb", "tr", "ti", "a", "a2"))
        bxc, bxcb, btr, bti, ba, ba2 = (bSL[k][sl] for k in ("xc", "xcb", "tr", "ti", "a", "a2"))
        j = d * 4 + cc
        pr, pi_ = (0, 1) if (cc % 2 == 0) else (2, 3)
        one = col(PC_ONE, 0)
        S.op("act", lambda e: e.activation(out=tr, in_=psum[pr], func=AF.Exp, bias=col(MC_NBR, j), scale=-1.0),
             reads=[b_ps[pr], b_pcols], writes=[btr])
        S.op("act", lambda e: e.activation(out=tr, in_=tr, func=AF.Ln, bias=one, scale=1.0), reads=[btr, b_pcols], writes=[btr])
        S.op("act", lambda e: e.activation(out=tr, in_=tr, func=AF.Exp, scale=-1.0), reads=[btr], writes=[btr])
        S.op("act", lambda e: e.activation(out=a_, in_=tr, func=AF.Exp, scale=col(MC_HC, j)), reads=[btr, b_pcols], writes=[ba])
        S.op("act", lambda e: e.activation(out=a2, in_=a_, func=AF.Square), reads=[ba], writes=[ba2])
        S.op("act", lambda e: e.activation(out=ti, in_=psum[pi_], func=AF.Exp, bias=col(MC_NBI, j), scale=-1.0),
             reads=[b_ps[pi_], b_pcols], writes=[bti])
        S.op("act", lambda e: e.activation(out=ti, in_=ti, func=AF.Ln, bias=one, scale=1.0), reads=[bti, b_pcols], writes=[bti])
        S.op("act", lambda e: e.activation(out=ti, in_=ti, func=AF.Exp, scale=-1.0), reads=[bti], writes=[bti])

    def lru_stage_b(l, t, cc, d, sl, SL, bSL, xco=None):
        xc, xcb, tr, ti, a_, a2 = (SL[k][sl] for k in ("xc", "xcb", "tr", "ti", "a", "a2"))
        bxc, bxcb, btr, bti, ba, ba2 = (bSL[k][sl] for k in ("xc", "xcb", "tr", "ti", "a", "a2"))
        if xco is not None:
            xc, bxc = xco
        j = d * 4 + cc
        one = col(PC_ONE, 0)
        S.op("dve", lambda e: e.tensor_scalar(out=a2, in0=a2, scalar1=0.99999988, scalar2=None, op0=ALU.min),
             reads=[ba2], writes=[ba2])
        S.op("dve", lambda e: e.tensor_tensor(out=ti, in0=ti, in1=xc, op=ALU.mult), reads=[bti, bxc], writes=[bti])
        S.op("act", lambda e: e.activation(out=a2, in_=a2, func=AF.Ln, bias=one, scale=-1.0), reads=[ba2, b_pcols], writes=[ba2])
        S.op("act", lambda e: e.activation(out=a2, in_=a2, func=AF.Exp, scale=0.5), reads=[ba2], writes=[ba2])

    def lru_stage_b2(l, t, cc, d, sl, SL, bSL, xco=None):
        xc, xcb, tr, ti, a_, a2 = (SL[k][sl] for k in ("xc", "xcb", "tr", "ti", "a", "a2"))
        bxc, bxcb, btr, bti, ba, ba2 = (bSL[k][sl] for k in ("xc", "xcb", "tr", "ti", "a", "a2"))
        j = d * 4 + cc
        if d == 0:
            S.op("dve", lambda e: e.tensor_tensor(out=a2, in0=a2, in1=ti, op=ALU.mult), reads=[ba2, bti], writes=[ba2])
        else:
            S.op("dve", lambda e: e.tensor_tensor(out=a2[:, ::-1], in0=a2[:, ::-1], in1=ti[:, ::-1], op=ALU.mult),
                 reads=[ba2, bti], writes=[ba2])
        car = col(PC_CAR, j)
        if t == 0:
            for s_ in ((0, 1) if d == 0 else (1, 0)):
                cs = slice(s_ * 256, (s_ + 1) * 256)
                if d == 0:
                    S.op("dve", lambda e, cs=cs: e.tensor_tensor_scan(out=tr[:, cs], data0=a_[:, cs], data1=a2[:, cs], initial=0.0,
                                                                       op0=ALU.mult, op1=ALU.add), reads=[ba, ba2], writes=[btr])
                    fin = tr[:, s_ * 256 + 255: s_ * 256 + 256]
                else:
                    S.op("dve", lambda e, cs=cs: e.tensor_tensor_scan(
                        out=tr[:, cs][:, ::-1], data0=a_[:, cs][:, ::-1], data1=a2[:, cs][:, ::-1], initial=0.0,
                        op0=ALU.mult, op1=ALU.add), reads=[ba, ba2], writes=[btr])
                    fin = tr[:, s_ * 256: s_ * 256 + 1]
                nscol = ((s_ * DEPTH + l) * 2 + d) * 4 + cc
                S.op("pool", lambda e, fin=fin, nscol=nscol: e.tensor_copy(out=nsT[:, nscol:nscol + 1], in_=fin),
                     reads=[btr], writes=[b_nsT])
        else:
            first = (t == 1) if d == 0 else (t == NTILES - 1)
            init = col(PC_H0, (l * 2 + d) * 4 + cc) if first else car
            if d == 0:
                S.op("dve", lambda e, init=init: e.tensor_tensor_scan(out=tr, data0=a_, data1=a2, initial=init,
                                                                       op0=ALU.mult, op1=ALU.add),
                     reads=[ba, ba2, b_pcols, b_car[j]], writes=[btr])
                fin = tr[:, NT - 1:NT]
            else:
                S.op("dve", lambda e, init=init: e.tensor_tensor_scan(out=tr[:, ::-1], data0=a_[:, ::-1], data1=a2[:, ::-1],
                                                                       initial=init, op0=ALU.mult, op1=ALU.add),
                     reads=[ba, ba2, b_pcols, b_car[j]], writes=[btr])
                fin = tr[:, 0:1]
            S.op("pool", lambda e, fin=fin, car=car: e.tensor_copy(out=car, in_=fin), reads=[btr], writes=[b_car[j]])

    def make_slots(places, tag, with_xc=True):
        SL = {k: [] for k in ("xc", "xcb", "tr", "ti", "a", "a2")}
        bSL = {k: [] for k in SL}
        for sl, (ar, off) in enumerate(places):
            o_ = off
            if with_xc:
                SL["xc"].append(ar.view(o_, [NT], F32)); o_ += 2048
            else:
                SL["xc"].append(None)
            SL["xcb"].append(ar.view(o_, [NT], BF16)); o_ += 1024
            for k in ("tr", "ti", "a", "a2"):
                SL[k].append(ar.view(o_, [NT], F32)); o_ += 2048
            for k in SL:
                bSL[k].append(Buf(f"{tag}{k}{sl}"))
        return SL, bSL

    def slot_bufs(bSL):
        return [x for k in bSL for x in bSL[k]]

    def mixer_s2(l, interleave):
        urw = [actv.view(i * 2080, [520], F32) for i in range(6)]
        b_urw = [[Buf(f"urw{i}_{s}") for s in range(2)] for i in range(6)]
        SL, bSL = make_slots([(wbig, WD_OFF), (wbig, WD_OFF + 11264), (wbig, WD_OFF + 22528)], "s2")
        claim("norm", [x for p in b_urw for x in p])
        claim("wd", slot_bufs(bSL))

        def load_window(idx):
            t, cc = divmod(idx, 4)
            w = idx % 6
            if t == 0:
                for s_ in range(2):
                    S.dma("sp", urw[w][:, s_ * 259: s_ * 259 + 259], urecT[cc * 128:(cc + 1) * 128, s_ * 259: s_ * 259 + 259],
                          reads=[b_urp[0][cc][s_]] + b_upads, writes=[b_urw[w][s_]])
            else:
                b0 = UBASE_S + (t - 1) * NT
                rd = [b_urp[t][cc][0]] + b_upads
                if t > 1:
                    rd.append(b_urp[t - 1][cc][0])
                if t < NTILES - 1:
                    rd.append(b_urp[t + 1][cc][0])
                S.dma("sp", urw[w][:, 0:515], urecT[cc * 128:(cc + 1) * 128, b0:b0 + 515], reads=rd, writes=b_urw[w])

        def pre(t, cc, sl):
            w = (t * 4 + cc) % 6
            rds = b_urw[w]
            xc = SL["xc"][sl]
            bxc = bSL["xc"][sl]
            if t == 0:
                W = lambda k, w=w: urw[w][:, 0:518].rearrange("p (s n) -> p s n", s=2)[:, :, k:k + 256]
                XO = xc.rearrange("p (s n) -> p s n", s=2)
            else:
                W = lambda k, w=w: urw[w][:, k:k + NT]
                XO = xc
            S.op("dve", lambda e, W=W, XO=XO, cc=cc: e.tensor_scalar(
                out=XO, in0=W(0), scalar1=col(MC_CW, 0 * 4 + cc), scalar2=col(MC_CB, cc), op0=ALU.mult, op1=ALU.add),
                reads=rds + [b_pcols], writes=[bxc])
            for k in range(1, 4):
                S.op("dve", lambda e, W=W, XO=XO, cc=cc, k=k: e.scalar_tensor_tensor(
                    out=XO, in0=W(k), scalar=col(MC_CW, k * 4 + cc), in1=XO, op0=ALU.mult, op1=ALU.add),
                    reads=rds + [b_pcols, bxc], writes=[bxc])
            S.dma("sp", xcT[cc * 128:(cc + 1) * 128, tile_cols(t)], xc, reads=[bxc], writes=[b_xc[t][cc]])

        def post(t, cc, sl):
            S.dma("sp", yfT[cc * 128:(cc + 1) * 128, tile_cols(t)], SL["tr"][sl], reads=[bSL["tr"][sl]], writes=[b_yf[t][cc]])

        NPF = 3
        for idx in range(NPF):
            load_window(idx)
        NIT = 4 * NTILES
        for idx in range(NIT + 2):
            if idx < NIT:
                t, cc = divmod(idx, 4)
                if idx + NPF < NIT:
                    load_window(idx + NPF)
                pre(t, cc, idx % 3)
                lru_stage_a(l, t, cc, 0, idx % 3, SL, bSL)
            if 0 <= idx - 1 < NIT:
                pt, pcc = divmod(idx - 1, 4)
                lru_stage_b(l, pt, pcc, 0, (idx - 1) % 3, SL, bSL)
            if idx < NIT:
                lru_stage_a2(l, t, cc, 0, idx % 3, SL, bSL)
            if 0 <= idx - 2 < NIT:
                qt, qcc = divmod(idx - 2, 4)
                lru_stage_b2(l, qt, qcc, 0, (idx - 2) % 3, SL, bSL)
                post(qt, qcc, (idx - 2) % 3)
            if idx < NIT and interleave is not None:
                interleave(idx)

    def mixer_s3(l, interleave=None):
        WD = WD_OFF
        SL, bSL = make_slots([(wbig, WD), (wbig, WD + 9216), (wbig, WD + 18432)], "s3", with_xc=False)
        xrr3 = [wbig.view(WD + 27648 + i * 2048, [NT], F32) for i in range(2)] + [actv.view(62464, [NT], F32)]
        xnew3 = [wbig.view(WD + 31744 + i * 2048, [NT], F32) for i in range(2)]
        NS = 4
        yfb = [actv.view(i * 2048, [NT], F32) for i in range(NS)]
        xcl = [actv.view(8192 + i * 2048, [NT], F32) for i in range(NS)]
        ggb = [actv.view(54272 + i * 2048, [NT], F32) for i in range(NS)]
        po2 = [po_sb, actv.view(MIX_OFF + 29696, [4, NT], BF16)]
        rec2 = [rec_sb, actv.view(MIX_OFF + 29696 + 4096, [4, NT], BF16)]
        b_yfb = [Buf(f"yfb{i}") for i in range(NS)]
        b_ggb = [Buf(f"ggb{i}") for i in range(NS)]
        b_xcl = [Buf(f"xcl{i}") for i in range(NS)]
        b_xrr3 = [Buf("xrr3_0"), Buf("xrr3_1"), Buf("xrr3_2")]
        b_xnew3 = [Buf("xnew3_0"), Buf("xnew3_1")]
        b_po2 = [b_posb, [Buf(f"po2_{g}") for g in range(4)]]
        b_rec2 = [b_recsb, Buf("rec2")]
        claim("wd", slot_bufs(bSL) + b_xrr3[:2] + b_xnew3)
        claim("norm", b_yfb + b_xcl)
        claim("mid", b_po2[1] + [b_rec2[1]] + b_ggb + b_xrr3[2:])
        items = [(t, cc) for t in range(NTILES - 1, -1, -1) for cc in range(4)]

        def loads(idx):
            t, cc = items[idx]
            k = idx % NS
            S.dma("sp", xcl[k], xcT[cc * 128:(cc + 1) * 128, tile_cols(t)], reads=[b_xc[t][cc]], writes=[b_xcl[k]])
            S.dma("sp", yfb[k], yfT[cc * 128:(cc + 1) * 128, tile_cols(t)], reads=[b_yf[t][cc]], writes=[b_yfb[k]])
            S.dma("sp", ggb[k], ggT[cc * 128:(cc + 1) * 128, tile_cols(t)], reads=[b_gg[t][cc]], writes=[b_ggb[k]])


        def post(idx, sl):
            t, cc = items[idx]
            k = idx % NS
            par = (idx // 4) % 2
            tr, btr = SL["tr"][sl], bSL["tr"][sl]
            ysum, bys = SL["ti"][sl], bSL["ti"][sl]
            S.op("dve", lambda e: e.tensor_tensor(out=ysum[:, ::-1], in0=tr[:, ::-1], in1=yfb[k][:, ::-1], op=ALU.add),
                 reads=[btr, b_yfb[k]], writes=[bys])
            S.op("dve", lambda e: e.tensor_tensor(out=rec2[par][:, cc, :][:, ::-1], in0=ysum[:, ::-1], in1=ggb[k][:, ::-1], op=ALU.mult),
                 reads=[bys, b_ggb[k]], writes=[b_rec2[par]])

        wq, wdone = [], []

        def wout_enqueue(t, par):
            wq.extend((t, par, d) for d in range(KC))

        def wout_step(n_new=cfg.get("wout_n", 2)):
            for (t, par, d) in wdone:
                res_apply(t, d, 4 + d % 2, 1, 0 if t == 0 else 1, xrr3, b_xrr3, xnew3, b_xnew3)
            wdone[:] = []
            for _ in range(n_new):
                if not wq:
                    break
                t, par, d = wq.pop(0)
                pd = 4 + d % 2
                if d < len(xrr3):
                    res_load(t, d, xrr3, b_xrr3)
                S.group("pe", [lambda e, k=k, d=d, pd=pd, par=par: e.matmul(
                    psum[pd], lhsT=wout_sb[:, k, d * 128:(d + 1) * 128],
                    rhs=(po2[par][:, k, :] if k < 4 else rec2[par][:, k - 4, :]),
                    start=(k == 0), stop=(k == KC - 1)) for k in range(KC)],
                    reads=b_wout2 + [b_rec2[par]] + b_po2[par], writes=[b_ps[pd]])
                wdone.append((t, par, d))

        NPF = 2
        for idx in range(NPF):
            loads(idx)
        NIT = len(items)
        xo = lambda i: (xcl[i % NS], b_xcl[i % NS])
        for idx in range(NIT + 2):
            if idx < NIT:
                t, cc = items[idx]
                lru_stage_a(l, t, cc, 1, idx % 3, SL, bSL, xco=xo(idx))
            if 0 <= idx - 1 < NIT:
                pt, pcc = items[idx - 1]
                lru_stage_b(l, pt, pcc, 1, (idx - 1) % 3, SL, bSL, xco=xo(idx - 1))
            if idx < NIT:
                lru_stage_a2(l, t, cc, 1, idx % 3, SL, bSL, xco=xo(idx))
            if 0 <= idx - 2 < NIT:
                qt, qcc = items[idx - 2]
                lru_stage_b2(l, qt, qcc, 1, (idx - 2) % 3, SL, bSL, xco=xo(idx - 2))
                post(idx - 2, (idx - 2) % 3)
            if idx + NPF < NIT:
                loads(idx + NPF)
            wout_step()
            if 0 <= idx - 2 < NIT and items[idx - 2][1] == 3:
                wout_enqueue(items[idx - 2][0], ((idx - 2) // 4) % 2)
            if 0 <= idx - 2 < NIT and items[idx - 2][1] == 0:
                tq = items[idx - 2][0]
                par = ((idx - 2) // 4) % 2
                for g in range(4):
                    S.dma("sp", po2[par][:, g, :], poT[g * 128:(g + 1) * 128, tile_cols(tq)], reads=[b_po[tq][g]],
                          writes=[b_po2[par][g]])
            if idx < NIT and interleave is not None:
                interleave(idx)
        while wq or wdone:
            wout_step()

    ws1_all = []
    for l in range(depth):
        w_ = Streamer()
        for it in gu_items(f1g, f1u, l) + wd_items(f1d, l):
            w_.add(*it)
        ws1_all.append(w_)
    initial_pass()
    zero_urec_pads()
    for l in range(depth):
        if do_ffn:
            stg_cur["bufs"], stg_cur["hz"] = stgw, b_stgw

            ws1 = ws1_all[l]
            ffn_phase(l, 0, wstream=ws1, n_first=max(0, 16 - ws1.n_issued))
        ws = Streamer()
        if do_ffn:
            for it in gu_items(f2g, f2u, l):
                ws.add(*it)
        if do_mixer:
            load_mixer_params(l)
            def il1(t, ws=ws, l=l):
                ws.pump(2)
                if t == 1:
                    ws.pump(0)
                    load_gate_weights(l)
            mixer_s1(l, il1, load_weights=lambda l=l: load_mixer_weights_s1(l))
            nxt_mod = (l + 1 < depth)
            if nxt_mod:
                mod_begin(l + 1)
            wo = [(wout_sb[:, k, :], w_out[l, k * 128:(k + 1) * 128, :], 1024, b_wout2) for k in range(KC)]
            ws.items[0:0] = wo
            first_s2 = [True]

            def il2(idx, ws=ws, l=l, nxt_mod=nxt_mod, first_s2=first_s2):
                if first_s2[0]:
                    claim("mid", b_wout2 + b_posb + [b_recsb])
                    first_s2[0] = False
                ws.pump(1)
                if nxt_mod:
                    mod_step(l + 1, idx)
            mixer_s2(l, il2)
            ws.drain()

            def il3(idx, l=l, nxt_mod=nxt_mod):
                if nxt_mod:
                    mod_step(l + 1, 36 + idx)
            mixer_s3(l, il3)
            if nxt_mod:
                mod_step(l + 1, NPIECE)
                mod_collect(l + 1)
        if do_ffn:
            ws.drain()
            ws2 = Streamer()
            for it in wd_items(f2d, l):
                ws2.add(*it)
            ffn_phase(l, 2, wstream=ws2, n_first=0, next_stream=(ws1_all[l + 1] if l + 1 < depth else None))
        if l + 1 < depth:
            if do_mixer:
                mod_derive(l + 1)
            else:
                compute_mod(l + 1)
    final_pass()

    with nc.Block() as block:
        S.emit(block)
    prog = {"nc": nc, "stack": stack, "S": S}
    return prog


def _pool_matrices():
    wins = (2, 4, 8, 16)
    out = np.zeros((20, 128, 128), np.float32)

    def full(L, w):
        t = np.arange(L)
        lo = np.clip(t - w // 2, 0, L)
        hi = np.clip(t + w // 2, 0, L)
        M = np.zeros((L, L), np.float64)
        for to in range(L):
            M[lo[to]:hi[to], to] = 1.0 / (hi[to] - lo[to])
            M[to, to] -= 1.0
        return M
    for g, w in enumerate(wins):
        M64 = full(64, w)
        out[g, 0:64, 0:64] = M64
        out[g, 64:128, 64:128] = M64
        M256 = full(256, w)
        for bi in range(2):
            for bo in range(2):
                out[4 + g * 4 + bi * 2 + bo] = M256[bi * 128:(bi + 1) * 128, bo * 128:(bo + 1) * 128]
    return np.ascontiguousarray(out.transpose(1, 0, 2))


_PROG_CACHE = {}


def _make_in_maps(inputs):
    f = lambda a: np.ascontiguousarray(np.asarray(a, dtype=np.float32))
    x_prompt = f(inputs["x_prompt"]); x_sample = f(inputs["x_sample"])
    state_lru = f(inputs["state_lru"]); c = f(inputs["c"]); c_ctx = f(inputs["c_ctx"])
    shared = {
        "eye": np.eye(128, dtype=np.float32),
        "poolm": _pool_matrices(),
        "norm_g": f(inputs["norm_g"]).reshape(DEPTH, 24, 128),
        "w_ada": f(inputs["w_ada"]),
        "b_ada": f(inputs["b_ada"]).reshape(DEPTH, 72, 128),
        "ffn1_gate": f(inputs["ffn1_gate"]), "ffn1_up": f(inputs["ffn1_up"]), "ffn1_down": f(inputs["ffn1_down"]),
        "w_in": f(inputs["w_in"]),
        "conv_w": f(inputs["conv_w"]).reshape(DEPTH, 16, 128),
        "conv_b": f(inputs["conv_b"]).reshape(DEPTH, 4, 128),
        "w_pool": f(inputs["w_pool"]),
        "pool_scale": f(inputs["pool_scale"]).reshape(DEPTH, 4, 128),
        "lru_w_r": f(inputs["lru_w_r"]), "lru_b_r": f(inputs["lru_b_r"]).reshape(DEPTH, 8, 128),
        "lru_w_i": f(inputs["lru_w_i"]), "lru_b_i": f(inputs["lru_b_i"]).reshape(DEPTH, 8, 128),
        "lru_lambda": f(inputs["lru_lambda"]).reshape(DEPTH, 8, 128),
        "w_out": f(inputs["w_out"]),
        "ffn2_gate": f(inputs["ffn2_gate"]), "ffn2_up": f(inputs["ffn2_up"]), "ffn2_down": f(inputs["ffn2_down"]),
        "final_g": f(inputs["final_g"]).reshape(8, 128),
    }
    in_maps = []
    for i in range(N_CORES):
        cc = np.empty((16, 128), np.float32)
        cc[0::2] = c_ctx.reshape(8, 128)
        cc[1::2] = c[i].reshape(8, 128)
        m = dict(shared)
        m["xp"] = np.ascontiguousarray(x_prompt[2 * i:2 * i + 2].reshape(512, D))
        m["xs"] = np.ascontiguousarray(x_sample[i])
        m["st"] = np.ascontiguousarray(state_lru[i].reshape(32, 128))
        m["cc"] = cc
        in_maps.append(m)
    return in_maps


def kernel(**inputs):
    if "prog" not in _PROG_CACHE:
        _PROG_CACHE["prog"] = build_program()
    prog = _PROG_CACHE["prog"]
    in_maps = _make_in_maps(inputs)
    res = run_bass_kernel_spmd(prog["nc"], in_maps, core_ids=list(range(N_CORES)))
    r = res.results
    y_prompt = np.stack([r[i]["yp"].reshape(2, 256, D) for i in range(N_CORES)], 0).reshape(16, 256, D)
    y_sample = np.stack([r[i]["ys"] for i in range(N_CORES)], 0)
    ns = np.stack([r[i]["ns"].reshape(2, DEPTH, 2, DLRU) for i in range(N_CORES)], 0).reshape(16, DEPTH, 2, DLRU)
    return (y_prompt.astype(np.float32), y_sample.astype(np.float32), ns.astype(np.float32))
```
